# Optimizing a Trainium2 kernel written in Bass

```python
import math, functools
import jax, jax.numpy as jnp
from jax import lax
import numpy as np

D_MODEL = 2048
BATCH = 8
SEQ = 2048
DEPTH = 2
DEC_BATCH = 128
DEC_SEQ = 4
PAST_LEN = 8192
PAGE_SIZE = 128

DK_R = 256
DV_R = 256
H_R = D_MODEL // DK_R
RET_CHUNK = 128
ROPE_BASE = 10000.0
HEAD_DIM = 128
H_A = D_MODEL // HEAD_DIM
KV_HEADS = H_A // 4
GROUP = H_A // KV_HEADS
WINDOW = 128
ATTN_BLOCK = WINDOW
D_FF = 4 * D_MODEL
EPS = 1e-6
NEG_INF = -1e30
SPLIT_SIZES = (H_R * DK_R, H_R * DK_R, H_R * DV_R, H_R * DV_R,
               H_A * HEAD_DIM, KV_HEADS * HEAD_DIM, KV_HEADS * HEAD_DIM,
               D_MODEL, D_MODEL)
IN_WIDTH = sum(SPLIT_SIZES)

kernel_name = "hybrid_retention_swa_sink_adaln_decoder_step"


def _rms(x, w=None):
    xf = x.astype(jnp.float32)
    y = xf * lax.rsqrt(jnp.mean(xf * xf, axis=-1, keepdims=True) + EPS)
    if w is not None:
        y = y * w.astype(jnp.float32)
    return y


def _rotary(t, pos):
    half = t.shape[-1] // 2
    inv = 1.0 / (ROPE_BASE ** (jnp.arange(half, dtype=jnp.float32) / half))
    ang = pos.astype(jnp.float32)[:, None] * inv[None, :]
    cos = jnp.cos(ang)[None, :, None, :]
    sin = jnp.sin(ang)[None, :, None, :]
    t1, t2 = t[..., :half], t[..., half:]
    return jnp.concatenate([t1 * cos - t2 * sin, t1 * sin + t2 * cos], axis=-1)


def _retention(q, k, v, s0, chunk):
    b, L, h, _ = q.shape
    n = L // chunk
    log_g = jnp.log1p(-jnp.exp2(-5.0 - jnp.arange(h, dtype=jnp.float32)))
    idx = jnp.arange(chunk, dtype=jnp.float32)
    diff = idx[:, None] - idx[None, :]
    causal = diff >= 0
    d_intra = jnp.where(causal, jnp.exp(jnp.where(causal, diff, 0.0) * log_g[:, None, None]), 0.0)
    q_dec = jnp.exp((idx + 1.0) * log_g[:, None])[..., None]
    k_dec = jnp.exp((chunk - 1.0 - idx) * log_g[:, None])[..., None]
    s_dec = jnp.exp(chunk * log_g)[:, None, None]

    def to_chunks(t):
        return t.reshape(b, n, chunk, h, t.shape[-1]).transpose(1, 0, 3, 2, 4)

    def step(s, inp):
        qi, ki, vi = inp
        att = jnp.einsum('bhid,bhjd->bhij', qi, ki) * d_intra
        o = (jnp.einsum('bhij,bhjv->bhiv', att, vi)
             + jnp.einsum('bhid,bhdv->bhiv', qi * q_dec, s))
        s = s * s_dec + jnp.einsum('bhjd,bhjv->bhdv', ki * k_dec, vi)
        return s, o

    s_final, oc = lax.scan(step, s0, (to_chunks(q), to_chunks(k), to_chunks(v)))
    o = oc.transpose(1, 0, 3, 2, 4).reshape(b, L, h, v.shape[-1])
    return o, s_final


def _sink_attend(q, k, v, mask, sinks):
    s = jnp.einsum('...qhgd,...shd->...hgqs', q, k,
                   preferred_element_type=jnp.float32) * (HEAD_DIM ** -0.5)
    s = jnp.where(mask, s, NEG_INF)
    sink = sinks.astype(jnp.float32).reshape(KV_HEADS, GROUP)[:, :, None, None]
    m = jnp.maximum(jnp.max(s, axis=-1, keepdims=True), sink)
    p = jnp.exp(s - m)
    p = p / (jnp.sum(p, axis=-1, keepdims=True) + jnp.exp(sink - m))
    return jnp.einsum('...hgqs,...shd->...qhgd', p.astype(v.dtype), v,
                      preferred_element_type=jnp.float32)


def _swa_prompt(q, k, v, sinks):
    b, s = q.shape[:2]
    nb = s // ATTN_BLOCK
    qb = q.reshape(b, nb, ATTN_BLOCK, KV_HEADS, GROUP, HEAD_DIM)

    def band(t):
        pad = jnp.zeros((b, ATTN_BLOCK) + t.shape[2:], t.dtype)
        tp = jnp.concatenate([pad, t], axis=1).reshape((b, nb + 1, ATTN_BLOCK) + t.shape[2:])
        return jnp.concatenate([tp[:, :-1], tp[:, 1:]], axis=2)

    kb, vb = band(k), band(v)
    blk = jnp.arange(nb)[:, None]
    qpos = blk * ATTN_BLOCK + jnp.arange(ATTN_BLOCK)[None, :]
    kpos = (blk - 1) * ATTN_BLOCK + jnp.arange(2 * ATTN_BLOCK)[None, :]
    diff = qpos[:, :, None] - kpos[:, None, :]
    mask = (diff >= 0) & (diff <= WINDOW) & (kpos[:, None, :] >= 0)
    o = _sink_attend(qb, kb, vb, mask[:, None, None], sinks)
    return o.reshape(b, s, H_A * HEAD_DIM)


def _mixer_prep(h, pos, w_in_l, qn, kn):
    b, L = h.shape[:2]
    pts = np.cumsum(SPLIT_SIZES)[:-1].tolist()
    qr, kr, vr, gr, qa, ka, va, mr, ma = jnp.split(h @ w_in_l, pts, axis=-1)
    qr = _rotary(qr.reshape(b, L, H_R, DK_R).astype(jnp.float32), pos)
    kr = _rotary(kr.reshape(b, L, H_R, DK_R).astype(jnp.float32), pos) * (DK_R ** -0.5)
    vr = vr.reshape(b, L, H_R, DV_R).astype(jnp.float32)
    qa = _rms(qa.reshape(b, L, KV_HEADS, GROUP, HEAD_DIM), qn).astype(h.dtype)
    ka = _rms(ka.reshape(b, L, KV_HEADS, HEAD_DIM), kn).astype(h.dtype)
    va = va.reshape(b, L, KV_HEADS, HEAD_DIM)
    return qr, kr, vr, gr, qa, ka, va, mr, ma


def _mix_prompt(qr, kr, vr, qa, ka, va, sk):
    b, s = qr.shape[:2]
    s0 = jnp.zeros((b, H_R, DK_R, DV_R), jnp.float32)
    o_r, s_new = _retention(qr, kr, vr, s0, RET_CHUNK)
    o_a = _swa_prompt(qa, ka, va, sk)
    keep = min(WINDOW, s)
    return o_r, o_a, (s_new, ka[:, s - keep:], va[:, s - keep:])


def _mix_sample(qr, kr, vr, qa, ka, va, sk, *, s0, k_buf, v_buf):
    b, L = qr.shape[:2]
    o_r, s_new = _retention(qr, kr, vr, s0.astype(jnp.float32), L)
    wb = k_buf.shape[1]
    k_all = jnp.concatenate([k_buf.astype(ka.dtype), ka], axis=1)
    v_all = jnp.concatenate([v_buf.astype(va.dtype), va], axis=1)
    qpos = PAST_LEN + jnp.arange(L)
    kpos = PAST_LEN - wb + jnp.arange(wb + L)
    diff = qpos[:, None] - kpos[None, :]
    mask = (diff >= 0) & (diff <= WINDOW)
    o_a = _sink_attend(qa, k_all, v_all, mask, sk).reshape(b, L, H_A * HEAD_DIM)
    return o_r, o_a, (s_new, k_all[:, L:], v_all[:, L:])


def _layer(x, c, pos, mix, ln1, ln2, wa, ba, wi, qn, kn, sk, wo, wu, wd):
    dt = x.dtype
    sh1, sc1, g1, sh2, sc2, g2 = [m[:, None, :] for m in
                                  jnp.split(jax.nn.silu(c) @ wa + ba, 6, axis=-1)]
    h = (_rms(x, ln1) * (1.0 + sc1) + sh1).astype(dt)
    qr, kr, vr, gr, qa, ka, va, mr, ma = _mixer_prep(h, pos, wi, qn, kn)
    o_r, o_a, new_state = mix(qr, kr, vr, qa, ka, va, sk)
    b, L = x.shape[:2]
    o_r = _rms(o_r).reshape(b, L, H_R * DV_R) * jax.nn.silu(gr.astype(jnp.float32))
    merged = (jax.nn.sigmoid(mr.astype(jnp.float32)) * o_r
              + jax.nn.sigmoid(ma.astype(jnp.float32)) * o_a)
    x = x + (g1 * (merged.astype(dt) @ wo)).astype(dt)
    h = (_rms(x, ln2) * (1.0 + sc2) + sh2).astype(dt)
    x = x + (g2 * (jnp.square(jax.nn.relu(h @ wu)) @ wd)).astype(dt)
    s_new, k_new, v_new = new_state
    return x, (s_new.astype(dt), k_new, v_new)


def setup_inputs(seed: int = 0) -> dict:
    key = jax.random.key(seed)
    ks = jax.random.split(key, 20)
    f32 = jnp.float32
    win_buf = min(WINDOW, PAST_LEN)

    def nrm(k, shape, scale):
        return jax.random.normal(k, shape, f32) * scale

    return {
        "x_prompt": nrm(ks[0], (BATCH, SEQ, D_MODEL), 1.0),
        "x_sample": nrm(ks[1], (DEC_BATCH, DEC_SEQ, D_MODEL), 1.0),
        "c_prompt": nrm(ks[2], (BATCH, D_MODEL), 1.0),
        "c_sample": nrm(ks[3], (DEC_BATCH, D_MODEL), 1.0),
        "state_ret": nrm(ks[4], (DEPTH, DEC_BATCH, H_R, DK_R, DV_R), 0.5),
        "cache_k_win": nrm(ks[5], (DEPTH, DEC_BATCH, win_buf, KV_HEADS, HEAD_DIM), 1.0),
        "cache_v_win": nrm(ks[6], (DEPTH, DEC_BATCH, win_buf, KV_HEADS, HEAD_DIM), 1.0),
        "norm1_w": 1.0 + nrm(ks[7], (DEPTH, D_MODEL), 0.02),
        "norm2_w": 1.0 + nrm(ks[8], (DEPTH, D_MODEL), 0.02),
        "w_ada": nrm(ks[9], (DEPTH, D_MODEL, 6 * D_MODEL), D_MODEL ** -0.5),
        "b_ada": nrm(ks[10], (DEPTH, 6 * D_MODEL), 0.01),
        "w_in": nrm(ks[11], (DEPTH, D_MODEL, IN_WIDTH), D_MODEL ** -0.5),
        "q_norm_w": 1.0 + nrm(ks[12], (DEPTH, HEAD_DIM), 0.02),
        "k_norm_w": 1.0 + nrm(ks[13], (DEPTH, HEAD_DIM), 0.02),
        "sinks": nrm(ks[14], (DEPTH, H_A), 0.5),
        "w_out": nrm(ks[15], (DEPTH, D_MODEL, D_MODEL), D_MODEL ** -0.5),
        "w_up": nrm(ks[16], (DEPTH, D_MODEL, D_FF), D_MODEL ** -0.5),
        "w_down": nrm(ks[17], (DEPTH, D_FF, D_MODEL), D_FF ** -0.5),
    }


def reference(x_prompt, x_sample, c_prompt, c_sample, state_ret, cache_k_win, cache_v_win,
              norm1_w, norm2_w, w_ada, b_ada, w_in, q_norm_w, k_norm_w, sinks,
              w_out, w_up, w_down):
    pos_p = jnp.arange(x_prompt.shape[1], dtype=jnp.int32)
    pos_s = PAST_LEN + jnp.arange(x_sample.shape[1], dtype=jnp.int32)
    xp, xs = x_prompt, x_sample
    rp, kp, vp, rs, kss, vss = [], [], [], [], [], []
    for l in range(DEPTH):
        wts = (norm1_w[l], norm2_w[l], w_ada[l], b_ada[l], w_in[l], q_norm_w[l],
               k_norm_w[l], sinks[l], w_out[l], w_up[l], w_down[l])
        xp, (r, kw, vw) = _layer(xp, c_prompt, pos_p, _mix_prompt, *wts)
        rp.append(r); kp.append(kw); vp.append(vw)
        mix_s = functools.partial(_mix_sample, s0=state_ret[l],
                                  k_buf=cache_k_win[l], v_buf=cache_v_win[l])
        xs, (r, kw, vw) = _layer(xs, c_sample, pos_s, mix_s, *wts)
        rs.append(r); kss.append(kw); vss.append(vw)
    return (xp, xs, jnp.stack(rp), jnp.stack(kp), jnp.stack(vp),
            jnp.stack(rs), jnp.stack(kss), jnp.stack(vss))
```

```python
from contextlib import ExitStack
import numpy as np
import concourse.bass as bass
import concourse.mybir as mybir
from concourse.bass_utils import run_bass_kernel_spmd

F32 = mybir.dt.float32
BF16 = mybir.dt.bfloat16
ALU = mybir.AluOpType
AF = mybir.ActivationFunctionType
AX = mybir.AxisListType

D = 2048
SEQ = 2048
NB = 16
LS = 4
PAST = 8192
DEPTH = 2
EPS = 1e-6
TP = 512
OQ, OK_, OV, OG, OQA, OKA, OVA, OMR, OMA = 0, 2048, 4096, 6144, 8192, 10240, 10752, 11264, 13312
INW = 15360


class Res:
    __slots__ = ("name", "w", "r", "sem", "cnt")

    def __init__(self, name):
        self.name = name
        self.w = None
        self.r = {}
        self.sem = None
        self.cnt = 0


class Op:
    __slots__ = ("eng", "fn", "deps", "idx", "mark", "dma", "markval")

    def __init__(self, eng, fn):
        self.eng = eng
        self.fn = fn
        self.deps = []
        self.mark = False
        self.dma = None
        self.markval = 0


class Prog:
    ENGS = ("pe", "act", "dve", "pool", "sp")

    def __init__(self, nc, stack):
        self.nc = nc
        self.stack = stack
        self.ops = {e: [] for e in self.ENGS}
        self.known = {e: {} for e in self.ENGS}
        self.esem = {e: stack.enter_context(nc.semaphore("s_" + e)) for e in self.ENGS}
        self.nres = 0
        self.allres = []

    def res(self, name=None):
        self.nres += 1
        r = Res(name or "r%d" % self.nres)
        self.allres.append(r)
        return r

    def _apply(self, op, need):
        eng = op.eng
        kn = self.known[eng]
        for key, val in need.items():
            if isinstance(key, str):
                if key == eng:
                    if eng == "pe" or len(self.ops[eng]) - val > 2:
                        continue
                if kn.get(key, -1) >= val:
                    continue
                kn[key] = val
                self.ops[key][val].mark = True
                op.deps.append(("e", key, val))
            else:
                if kn.get(key, -1) >= val:
                    continue
                kn[key] = val
                op.deps.append(("d", key, val))

    @staticmethod
    def _acc(need, tok):
        if tok is None:
            return
        k, v = tok
        if need.get(k, -1) < v:
            need[k] = v

    def op(self, eng, fn, reads=(), writes=(), dma_key=None):
        o = Op(eng, fn)
        o.idx = len(self.ops[eng])
        need = {}
        for r in reads:
            self._acc(need, r.w)
        for r in writes:
            self._acc(need, r.w)
            for k, v in r.r.items():
                self._acc(need, (k, v))
        self._apply(o, need)
        if dma_key is not None:
            if dma_key.sem is None:
                dma_key.sem = self.stack.enter_context(self.nc.semaphore("d_" + dma_key.name))
            dma_key.cnt += 1
            o.dma = dma_key
            tok = (dma_key, dma_key.cnt * 16)
        else:
            tok = (eng, o.idx)
        for r in reads:
            if r.r.get(tok[0], -1) < tok[1]:
                r.r[tok[0]] = tok[1]
        for r in writes:
            r.w = tok
            r.r = {}
        self.ops[eng].append(o)
        return o

    def barrier(self):
        last = {}
        for e in self.ENGS:
            if self.ops[e]:
                for o in reversed(self.ops[e]):
                    if o.fn is not None and o.dma is None:
                        last[e] = o.idx
                        break
        for e in self.ENGS:
            o = Op(e, None)
            o.idx = len(self.ops[e])
            need = {k: v for k, v in last.items() if k != e}
            for r in self.allres:
                if r.sem is not None and r.cnt:
                    need[r] = r.cnt * 16
            self._apply(o, need)
            self.ops[e].append(o)

    def emit(self):
        nc = self.nc
        for e in self.ENGS:
            c = 0
            for o in self.ops[e]:
                if o.mark:
                    c += 1
                    o.markval = c
        self.stats = {e: (len(self.ops[e]), sum(1 for o in self.ops[e] if o.mark),
                          sum(len(o.deps) for o in self.ops[e])) for e in self.ENGS}

        def run(e, E):
            for o in self.ops[e]:
                for tok in o.deps:
                    if tok[0] == "e":
                        E.wait_ge(self.esem[tok[1]], self.ops[tok[1]][tok[2]].markval)
                    else:
                        E.wait_ge(tok[1].sem, tok[2])
                if o.fn is None:
                    continue
                ins = o.fn(E)
                if o.dma is not None:
                    ins.then_inc(o.dma.sem, 16)
                elif o.mark:
                    ins.then_inc(self.esem[e], 1)

        with nc.Block() as block:
            @block.tensor
            def _(E):
                run("pe", E)

            @block.scalar
            def _(E):
                run("act", E)

            @block.vector
            def _(E):
                run("dve", E)

            @block.gpsimd
            def _(E):
                run("pool", E)

            @block.sync
            def _(E):
                run("sp", E)


def _consts():
    half = 128
    inv = (1.0 / (10000.0 ** (np.arange(half, dtype=np.float32) / np.float32(half)))).astype(np.float32)
    pos_p = np.arange(SEQ, dtype=np.float32)
    ang = (pos_p[:, None] * inv[None, :]).astype(np.float32)
    cs_p = np.stack([np.cos(ang), np.sin(ang)], 0).astype(np.float32)
    pos_s = (PAST + (np.arange(NB * LS) % LS)).astype(np.float32)
    ang = (pos_s[:, None] * inv[None, :]).astype(np.float32)
    cs_s = np.stack([np.cos(ang), np.sin(ang)], 0).astype(np.float32)
    h = np.arange(8, dtype=np.float64)
    log_g = np.log1p(-np.exp2(-5.0 - h))
    i = np.arange(128, dtype=np.float64)
    dec = np.zeros((4, 128, 8), np.float32)
    dec[0] = np.exp((i[:, None] + 1.0) * log_g[None, :])
    dec[1] = np.exp(-(i[:, None] + 1.0) * log_g[None, :]) / 16.0
    dec[2] = np.exp((127.0 - i[:, None]) * log_g[None, :]) / 16.0
    dec[3] = np.exp(128.0 * log_g)[None, :]
    t = (np.arange(128) % LS).astype(np.float64)
    decs = np.zeros((4, 128, 8), np.float32)
    decs[0] = np.exp((t[:, None] + 1.0) * log_g[None, :])
    decs[1] = np.exp(-(t[:, None] + 1.0) * log_g[None, :]) / 16.0
    decs[2] = np.exp((LS - 1.0 - t[:, None]) * log_g[None, :]) / 16.0
    decs[3] = np.exp(float(LS) * log_g)[None, :]
    j = np.arange(128)
    m = np.zeros((6, 128, 128), np.float32)
    m[0] = (j[:, None] <= j[None, :])
    m[1] = (j[:, None] >= j[None, :])
    bb = j // LS
    tt = j % LS
    m[2] = (bb[:, None] == bb[None, :]) & (tt[:, None] <= tt[None, :])
    m[3] = (j[:, None] >= tt[None, :])
    m[4] = np.eye(128)
    m[5] = 1.0
    bm = np.zeros((128, 16), np.float32)
    bm[:64] = (bb[:64, None] == np.arange(16)[None, :])
    bmT = np.zeros((128, 16, 64), np.float32)
    bmT[:] = (np.arange(16)[:, None] == bb[None, :64])[None]
    return dict(cs_p=cs_p, cs_s=cs_s, dec=dec, decs=decs, masks=m, bm=bm, bmT=bmT)


def build_nc():
    nc = bass.Bass("TRN2", target_bir_lowering=False)

    def din(name, shape, dt=F32):
        return nc.dram_tensor(name, list(shape), dt, kind="ExternalInput").ap()

    def dout(name, shape):
        return nc.dram_tensor(name, list(shape), F32, kind="ExternalOutput").ap()

    xp = din("xp", [SEQ, D])
    xs = din("xs", [NB * LS, D])
    cv = din("cv", [NB + 1, D])
    st_in = din("st_in", [DEPTH, NB, 8, 256, 256])
    ck_in = din("ck_in", [DEPTH, NB, 128, 4, 128])
    cv_in = din("cv_in", [DEPTH, NB, 128, 4, 128])
    n1w = din("n1w", [DEPTH, 16, 128])
    n2w = din("n2w", [DEPTH, 16, 128])
    w_ada = din("w_ada", [DEPTH, D, 6 * D])
    b_ada = din("b_ada", [DEPTH, 96, 128])
    w_in = din("w_in", [DEPTH, D, INW])
    qnw = din("qnw", [DEPTH, 128])
    knw = din("knw", [DEPTH, 128])
    sinks = din("sinks", [DEPTH, 16])
    w_out = din("w_out", [DEPTH, D, D])
    w_up = din("w_up", [DEPTH, D, 4 * D])
    w_down = din("w_down", [DEPTH, 4 * D, D])
    cs_p = din("cs_p", [2, SEQ, 128])
    cs_s = din("cs_s", [2, NB * LS, 128])
    dec_d = din("dec", [4, 128, 8])
    decs_d = din("decs", [4, 128, 8])
    masks_d = din("masks", [6, 128, 128])
    bm_d = din("bm", [128, 16])
    bmT_d = din("bmT", [128, 16, 64])

    wbfL = [nc.dram_tensor("wbf%d" % i, [68, 128, 8192], BF16, kind="Internal").ap() for i in range(2)]

    class _W:
        def __getitem__(self, tid):
            return wbfL[tid // 68][tid % 68]
    wbf = _W()
    yp = dout("yp", [SEQ, D])
    ys = dout("ys", [NB * LS, D])
    st_p = dout("st_p", [DEPTH, 8, 256, 256])
    ckp = dout("ckp", [DEPTH, 128, 4, 128])
    cvp = dout("cvp", [DEPTH, 128, 4, 128])
    st_s = dout("st_s", [DEPTH, NB, 8, 256, 256])
    cks = dout("cks", [DEPTH, NB, 128, 4, 128])
    cvs = dout("cvs", [DEPTH, NB, 128, 4, 128])

    with ExitStack() as st:
        P = Prog(nc, st)
        AW = 52000
        arena = st.enter_context(nc.sbuf_tensor("arena", [128, AW], F32))
        psb = [st.enter_context(nc.psum_tensor("ps%d" % i, [128, 512], F32)) for i in range(8)]
        psr = [P.res("ps%d" % i) for i in range(8)]
        pctr = [0]

        def nb():
            i = pctr[0] % 7
            pctr[0] += 1
            return psb[i], psr[i]

        off = [0]

        def carve(nwords, dt=F32):
            a = off[0]
            off[0] += nwords
            assert off[0] <= AW, off[0]
            v = arena[:, a:a + nwords]
            return v.bitcast(BF16) if dt == BF16 else v

        def mm(out, lhsT, rhs, start, stop, reads, writes):
            P.op("pe", lambda E: E.matmul(out, lhsT=lhsT, rhs=rhs, start=start, stop=stop), reads, writes)

        def tr(out, in_, ident, reads, writes):
            P.op("pe", lambda E: E.transpose(out=out, in_=in_, identity=ident), reads, writes)

        def act(out, in_, func, reads, writes, **kw):
            P.op("act", lambda E: E.activation(out=out, in_=in_, func=func, **kw), reads, writes)

        def tt(out, in0, in1, op, reads, writes, eng="dve"):
            P.op(eng, lambda E: E.tensor_tensor(out=out, in0=in0, in1=in1, op=op), reads, writes)

        def ts(out, in0, s1, s2, op0, op1, reads, writes, eng="dve"):
            if op1 is None:
                P.op(eng, lambda E: E.tensor_scalar(out=out, in0=in0, scalar1=s1, scalar2=None, op0=op0), reads, writes)
            else:
                P.op(eng, lambda E: E.tensor_scalar(out=out, in0=in0, scalar1=s1, scalar2=s2, op0=op0, op1=op1), reads, writes)

        def stt(out, in0, scalar, in1, op0, op1, reads, writes):
            P.op("dve", lambda E: E.scalar_tensor_tensor(out=out, in0=in0, scalar=scalar, in1=in1, op0=op0, op1=op1), reads, writes)

        def cp(out, in_, reads, writes, eng="dve"):
            if eng == "act":
                P.op("act", lambda E: E.copy(out=out, in_=in_), reads, writes)
            else:
                P.op(eng, lambda E: E.tensor_copy(out=out, in_=in_), reads, writes)

        def recip(out, in_, reads, writes):
            P.op("dve", lambda E: E.reciprocal(out=out, in_=in_), reads, writes)

        def dma(q, out, in_, reads, writes, key):
            P.op(q, lambda E: E.dma_start(out=out, in_=in_), reads, writes, dma_key=key)

        def rstd_from(out, in_, n, reads, writes, tmp, r_tmp):
            act(tmp, in_, AF.Sqrt, reads, [r_tmp], bias=epsc[:in_.shape[0], :], scale=1.0 / n)
            recip(out, tmp, [r_tmp], writes)

        ident = carve(128)
        identb = carve(64, BF16)
        onesb = carve(64, BF16)
        mk = [carve(64, BF16) for _ in range(4)]
        mstage = carve(128)
        epsc = carve(1)
        dec = carve(32).rearrange("p (a h) -> p a h", a=4)
        decs = carve(32).rearrange("p (a h) -> p a h", a=4)
        bm = carve(16)
        bmTb = carve(16 * 64 // 2, BF16).rearrange("p (b c) -> p b c", b=16)
        bmb = carve(8, BF16)
        mod = carve(DEPTH * 96 * 17).rearrange("p (l c b) -> p l c b", l=DEPTH, c=96)
        A = carve(DEPTH * 2 * 16 * 17).rearrange("p (l s k b) -> p l s k b", l=DEPTH, s=2, k=16)
        lnw = carve(DEPTH * 2 * 16).rearrange("p (l s k) -> p l s k", l=DEPTH, s=2)
        scT = carve(16 * 18 // 2, BF16).rearrange("p (k b) -> p k b", k=16)
        qkw = carve(DEPTH * 2 * 128).rearrange("p (l s d) -> p l s d", l=DEPTH, s=2)
        esink = carve(DEPTH * 16).rearrange("p (l h) -> p l h", l=DEPTH)
        kprev = carve(DEPTH * 4 * 128 // 2, BF16).rearrange("p (l u k) -> p l u k", l=DEPTH, u=4)
        vprev = carve(DEPTH * 4 * 130 // 2, BF16).rearrange("p (l u k) -> p l u k", l=DEPTH, u=4)
        r_const = P.res("const")
        r_mod = P.res("mod")
        r_carry = P.res("carry")
        persist_end = off[0]

        xT = carve(16 * TP).rearrange("p (k t) -> p k t", k=16)
        hT = carve(16 * TP // 2, BF16).rearrange("p (k t) -> p k t", k=16)
        mg = carve(4 * 2048 // 2, BF16).rearrange("p (t c) -> p t c", t=4)
        wts = [carve(16 * 512 // 2, BF16).rearrange("p (k c) -> p k c", k=16) for _ in range(2)]
        r_mg = P.res("mg")
        r_xk = [P.res("x%d" % i) for i in range(16)]
        r_hk = [P.res("h%d" % i) for i in range(16)]
        r_w = [P.res("w0"), P.res("w1")]
        sgm = carve(4 * 512).rearrange("p (t c) -> p t c", t=4)
        r_sgm = P.res("sgm")
        xin = sgm.rearrange("p t c -> p (t c)")
        r_xin = r_sgm
        r_rotA, r_rotB = P.res("rotA"), P.res("rotB")
        r_bf = (P.res("bf0"), P.res("bf1"))
        tmp = [carve(512) for _ in range(6)]
        r_tmp = [P.res("t%d" % i) for i in range(6)]
        sm = [carve(8) for _ in range(6)]
        r_sm = [P.res("sm%d" % i) for i in range(6)]
        sq = carve(2 * TP // 2, BF16).rearrange("p (k t) -> p k t", k=2)
        r_sq = P.res("sq")
        rsd = carve(TP)
        r_rsd = P.res("rsd")
        cst = carve(2 * 4 * 128).rearrange("p (a t d) -> p a t d", a=2, t=4)
        r_cst = P.res("cst")
        ub = off[0]
        qdT = carve(4 * TP // 2, BF16).rearrange("p (c t) -> p c t", c=4)
        kdT = carve(4 * TP // 2, BF16).rearrange("p (c t) -> p c t", c=4)
        kd = carve(4 * 512 // 2, BF16).rearrange("p (t c) -> p t c", t=4)
        vv = carve(4 * 512 // 2, BF16).rearrange("p (t c) -> p t c", t=4)
        gm = carve(4 * 512).rearrange("p (t c) -> p t c", t=4)
        r_qdT, r_kdT, r_kd, r_vv, r_gm = P.res("qdT"), P.res("kdT"), P.res("kd"), P.res("vv"), P.res("gm")
        kT = carve((128 + TP) // 2, BF16)
        vext = carve(5 * 130 // 2, BF16).rearrange("p (t c) -> p t c", t=5)
        r_kT, r_vext = P.res("kT"), P.res("vext")
        S = carve(2 * 256).rearrange("p (c v) -> p c v", c=2)
        Sb = carve(2 * 256 // 2, BF16).rearrange("p (c v) -> p c v", c=2)
        r_S, r_Sb = P.res("S"), P.res("Sb")
        pbuf = [carve(512 // 2, BF16) for _ in range(4)]
        r_pbuf = [P.res("pb%d" % i) for i in range(4)]
        attb = [carve(64, BF16) for _ in range(2)]
        r_attb = [P.res("attb0"), P.res("attb1")]
        qaT = carve(4 * TP // 2, BF16).rearrange("p (c t) -> p c t", c=4)
        r_qaT = P.res("qaT")
        ug = [qaT] * 2
        r_ug = [r_qaT] * 2
        qexp = [carve(2 * 64 // 2, BF16).rearrange("p (c t) -> p c t", c=2) for _ in range(2)]
        kexp = [carve(256 // 2, BF16) for _ in range(2)]
        pexp = [carve(64 // 2, BF16) for _ in range(2)]
        r_qexp = [P.res("qexp0"), P.res("qexp1")]
        r_kexp = [P.res("kexp0"), P.res("kexp1")]
        r_pexp = [P.res("pexp0"), P.res("pexp1")]
        s0 = [carve(512).rearrange("p (c v) -> p c v", c=2) for _ in range(2)]
        s0b = [carve(256, BF16).rearrange("p (c v) -> p c v", c=2) for _ in range(2)]
        r_s0 = [P.res("s0a"), P.res("s0b")]
        r_s0b = [P.res("s0ba"), P.res("s0bb")]
        cst_k = [carve(128) for _ in range(2)]
        cst_v = [carve(128) for _ in range(2)]
        r_cstk = [P.res("cka"), P.res("ckb")]
        r_cstv = [P.res("cva"), P.res("cvb")]
        kcb = [carve(64, BF16) for _ in range(2)]
        kcT = [carve(64, BF16) for _ in range(2)]
        r_kcb = [P.res("kcb0"), P.res("kcb1")]
        r_kcT = [P.res("kcT0"), P.res("kcT1")]
        vcall = mg[:, 1:4, :].rearrange("p t c -> p (t c)")[:, 0:16 * 130].rearrange("p (b c) -> p b c", b=16)
        r_vcall = P.res("vcall")
        kvst = carve(256)
        r_kvst = P.res("kvst")
        gfree = gm[:, 1:4, :].rearrange("p t c -> p (t c)")
        kfree = kd[:, 1:4, :].rearrange("p t c -> p (t c)")
        vfree = vv[:, 1:4, :].rearrange("p t c -> p (t c)")
        s0 = s0 + [gfree[:, 0:512].rearrange("p (c v) -> p c v", c=2), gfree[:, 512:1024].rearrange("p (c v) -> p c v", c=2)]
        r_s0 = r_s0 + [P.res("s0c"), P.res("s0d")]
        s0b = s0b + [kfree[:, 0:512].rearrange("p (c v) -> p c v", c=2), kfree[:, 512:1024].rearrange("p (c v) -> p c v", c=2)]
        r_s0b = r_s0b + [P.res("s0bc"), P.res("s0bd")]
        cst_k = cst_k + [gfree[:, 1024:1152], gfree[:, 1152:1280]]
        cst_v = cst_v + [gfree[:, 1280:1408], gfree[:, 1408:1536]]
        r_cstk = r_cstk + [P.res("ckc"), P.res("ckd")]
        r_cstv = r_cstv + [P.res("cvc"), P.res("cvd")]
        sfree = sgm[:, 1:4, :].rearrange("p t c -> p (t c)")
        s0 = s0 + [sfree[:, 0:512].rearrange("p (c v) -> p c v", c=2), sfree[:, 512:1024].rearrange("p (c v) -> p c v", c=2)]
        r_s0 = r_s0 + [P.res("s0e"), P.res("s0f")]
        s0b = s0b + [sfree[:, 1024:1280].bitcast(BF16).rearrange("p (c v) -> p c v", c=2),
                     sfree[:, 1280:1536].bitcast(BF16).rearrange("p (c v) -> p c v", c=2)]
        r_s0b = r_s0b + [P.res("s0be"), P.res("s0bf")]
        kcb = kcb + [vfree[:, 0:128], vfree[:, 128:256]]
        kcT = kcT + [vfree[:, 256:384], vfree[:, 384:512]]
        r_kcb = r_kcb + [P.res("kcb2"), P.res("kcb3")]
        r_kcT = r_kcT + [P.res("kcT2"), P.res("kcT3")]
        print("arena words used", off[0])

        r_dram = {}

        def rd(name):
            if name not in r_dram:
                r_dram[name] = P.res("dr_" + name)
            return r_dram[name]

        dma("sp", ident, masks_d[4], [], [r_const], r_const)
        cp(identb, ident, [r_const], [r_const])
        dma("sp", mstage, masks_d[5], [], [r_const], r_const)
        cp(onesb, mstage, [r_const], [r_const])
        for i in range(4):
            dma("sp", mstage, masks_d[i], [], [r_const], r_const)
            cp(mk[i], mstage, [r_const], [r_const])
        P.op("dve", lambda E: E.memset(epsc, EPS), [], [r_const])
        for a in range(4):
            dma("sp", dec[:, a, :], dec_d[a], [], [r_const], r_const)
            dma("sp", decs[:, a, :], decs_d[a], [], [r_const], r_const)
        dma("sp", bm, bm_d, [], [r_const], r_const)
        cp(bmb, bm, [r_const], [r_const])
        for b4 in range(4):
            dma("sp", xin[:, 0:256].rearrange("p (b c) -> p b c", b=4), bmT_d[:, b4 * 4:(b4 + 1) * 4, :], [], [r_xin], r_xin)
            cp(bmTb[:, b4 * 4:(b4 + 1) * 4, :], xin[:, 0:256].rearrange("p (b c) -> p b c", b=4), [r_xin], [r_const])
        for l in range(DEPTH):
            dma("sp", qkw[:, l, 0, :], qnw[l:l + 1, :].to_broadcast([128, 128]), [], [r_const], r_const)
            dma("sp", qkw[:, l, 1, :], knw[l:l + 1, :].to_broadcast([128, 128]), [], [r_const], r_const)
            dma("sp", esink[:, l, :], sinks[l:l + 1, :].to_broadcast([128, 16]), [], [r_const], r_const)
        act(esink, esink, AF.Exp, [r_const], [r_const])
        for l in range(DEPTH):
            for s_, src in enumerate((n1w, n2w)):
                dma("sp", xin[0:16, 0:128], src[l], [], [r_xin], r_xin)
                pb, pr = nb()
                tr(pb[:, 0:16], xin[0:16, 0:128], ident[0:16, 0:16], [r_xin, r_const], [pr])
                cp(lnw[:, l, s_, :], pb[:, 0:16], [pr], [r_mod])
        dma("sp", xin[0:17, :], cv, [], [r_xin], r_xin)
        act(tmp[0][0:17, :], xin[0:17, 0:512], AF.Sigmoid, [r_xin], [r_tmp[0]])
        cfull = xT[:, 0:4, :].rearrange("p k t -> p (k t)")
        r_cf = None
        act(cfull[0:17, :], xin[0:17, :], AF.Sigmoid, [r_xin], r_xk[0:4])
        tt(cfull[0:17, :], cfull[0:17, :], xin[0:17, :], ALU.mult, r_xk[0:4] + [r_xin], r_xk[0:4])
        for k in range(16):
            pb, pr = nb()
            tr(pb[:, 0:17], cfull[0:17, k * 128:(k + 1) * 128], ident[0:17, 0:17], r_xk[0:4] + [r_const], [pr])
            cp(scT[:, k, 0:17], pb[:, 0:17], [pr], [r_mod])
        wctr = [0]

        wmode = ["none"]
        wtid = [0]
        wb_pending = []

        def flush_wb():
            while wb_pending:
                s_, tid_, n_, down_ = wb_pending.pop(0)
                if down_:
                    dma("pool", wbf[tid_], wts[s_][:].rearrange("p k c -> p (k c)"), [r_w[s_]], [], r_w[s_])
                else:
                    dma("pool", wbf[tid_].rearrange("p (k c) -> p k c", k=16)[:, :, 0:n_], wts[s_][:, :, 0:n_], [r_w[s_]], [], r_w[s_])

        def wload(src_ap, ncols, extra=None, down=False):
            s = wctr[0] % 2
            wctr[0] += 1
            n = ncols if extra is None else 2 * ncols
            tid = wtid[0]
            if wmode[0] != "none":
                wtid[0] += 1
            mode = wmode[0]
            if mode == "p0":
                mode = "write" if tid % 2 == 0 else "cast"
            elif mode == "p1":
                mode = "read" if tid % 2 == 0 else "write"
            if mode == "read":
                if down:
                    dma("pool", wts[s][:].rearrange("p k c -> p (k c)"), wbf[tid], [], [r_w[s]], r_w[s])
                else:
                    dma("pool", wts[s][:, :, 0:n], wbf[tid].rearrange("p (k c) -> p k c", k=16)[:, :, 0:n], [], [r_w[s]], r_w[s])
                flush_wb()
                return s
            if down:
                wdv_ = wts[s][:].rearrange("p k c -> p (k c)").rearrange("p (f c) -> p f c", f=4)
                dma("pool", wdv_, src_ap.rearrange("(f p) c -> p f c", p=128), [], [r_w[s]], r_w[s])
            else:
                dma("pool", wts[s][:, :, 0:ncols], src_ap.rearrange("(k p) c -> p k c", p=128), [], [r_w[s]], r_w[s])
                if extra is not None:
                    dma("pool", wts[s][:, :, ncols:2 * ncols], extra.rearrange("(k p) c -> p k c", p=128), [], [r_w[s]], r_w[s])
            flush_wb()
            if mode == "write":
                wb_pending.append((s, tid, n, down))
            return s

        for l in range(DEPTH):
            dma("sp", xin[0:96, 0:128], b_ada[l], [], [r_xin], r_xin)
            pb, pr = nb()
            tr(pb[:, 0:96], xin[0:96, 0:128], ident[0:96, 0:96], [r_xin, r_const], [pr])
            cp(tmp[1][:, 0:96], pb[:, 0:96], [pr], [r_tmp[1]])
            for ct in range(24):
                s = wload(w_ada[l][:, ct * 512:(ct + 1) * 512], 512)
                pb, pr = nb()
                for c4 in range(4):
                    for k in range(16):
                        mm(pb[:, c4 * 17:(c4 + 1) * 17], wts[s][:, k, c4 * 128:(c4 + 1) * 128], scT[:, k, 0:17],
                           k == 0, k == 15, [r_w[s], r_mod], [pr])
                tt(mod[:, l, ct * 4:(ct + 1) * 4, :], pb[:, 0:68].rearrange("p (c b) -> p c b", c=4),
                   tmp[1][:, ct * 4:(ct + 1) * 4].unsqueeze(2).to_broadcast([128, 4, 17]), ALU.add,
                   [pr, r_tmp[1]], [r_mod])
            for s_ in range(2):
                kind = 1 if s_ == 0 else 4
                ts(A[:, l, s_], mod[:, l, kind * 16:(kind + 1) * 16, :], 1.0, None, ALU.add, None, [r_mod], [r_mod])
                tt(A[:, l, s_], A[:, l, s_], lnw[:, l, s_, :].unsqueeze(2).to_broadcast([128, 16, 17]), ALU.mult,
                   [r_mod], [r_mod])

        def run_pass(kind, tok0):
            samp = kind == "s"
            T = NB * LS if samp else TP
            M = 64 if samp else 128
            NT = 1 if samp else TP // 128
            xsrc = xs if samp else xp
            ydst = ys if samp else yp
            DEC = decs if samp else dec
            first = (not samp) and tok0 == 0
            last = (not samp) and tok0 + TP == SEQ

            for t in range(NT):
                dma("sp", xin[0:M, :], xsrc[tok0 + t * 128: tok0 + t * 128 + M, :], [], [r_xin], r_xin)
                for kb in range(4):
                    pb, pr = nb()
                    for k4 in range(4):
                        k = kb * 4 + k4
                        tr(pb[:, k4 * 128:k4 * 128 + M], xin[0:M, k * 128:(k + 1) * 128], ident[0:M, 0:M],
                           [r_xin, r_const], [pr])
                    act(xT[:, kb * 4:(kb + 1) * 4, t * 128:t * 128 + M],
                        pb[:].rearrange("p (k c) -> p k c", k=4)[:, :, 0:M], AF.Copy, [pr], r_xk[kb * 4:(kb + 1) * 4])
            for a in range(2):
                for t in range(NT):
                    src = cs_s[a] if samp else cs_p[a][tok0 + t * 128: tok0 + (t + 1) * 128, :]
                    dma("sp", cst[0:M, a, t, :], src, [], [r_cst], r_cst)

            def modulate(l, s_):
                shk = 0 if s_ == 0 else 3
                pb, pr = nb()
                for kb in range(8):
                    act(sq[:, :, 0:T], xT[:, kb * 2:(kb + 1) * 2, 0:T], AF.Square, r_xk[kb * 2:(kb + 1) * 2], [r_sq])
                    for k4 in range(2):
                        mm(pb[:, 0:T], onesb, sq[:, k4, 0:T], kb == 0 and k4 == 0, kb == 7 and k4 == 1,
                           [r_sq, r_const], [pr])
                rstd_from(rsd[:, 0:T], pb[:, 0:T], float(D), [pr, r_const], [r_rsd], tmp[0][:, 0:T], r_tmp[0])
                for k in range(16):
                    i = k % 2
                    tt(tmp[1 + i][:, 0:T], xT[:, k, 0:T], rsd[:, 0:T], ALU.mult, [r_xk[k], r_rsd], [r_tmp[1 + i]])
                    if not samp:
                        act(hT[:, k, 0:T], tmp[1 + i][:, 0:T], AF.Identity, [r_tmp[1 + i], r_mod], [r_hk[k]],
                            scale=A[:, l, s_, k, 0:1], bias=mod[:, l, shk * 16 + k, 0:1])
                    else:
                        v3 = tmp[1 + i][:, 0:T].rearrange("p (b t) -> p b t", t=LS)
                        tt(v3, v3, A[:, l, s_, k, 1:17].unsqueeze(2).to_broadcast([128, NB, LS]), ALU.mult,
                           [r_tmp[1 + i], r_mod], [r_tmp[1 + i]])
                        tt(hT[:, k, 0:T].rearrange("p (b t) -> p b t", t=LS), v3,
                           mod[:, l, shk * 16 + k, 1:17].unsqueeze(2).to_broadcast([128, NB, LS]), ALU.add,
                           [r_tmp[1 + i], r_mod], [r_hk[k]])

            def resid_add(l, gk, k, pb, pr):
                if not samp:
                    stt(xT[:, k, 0:T], pb[:, 0:T], mod[:, l, gk * 16 + k, 0:1], xT[:, k, 0:T], ALU.mult, ALU.add,
                        [pr, r_mod, r_xk[k]], [r_xk[k]])
                else:
                    v3 = tmp[4][:, 0:T].rearrange("p (b t) -> p b t", t=LS)
                    tt(v3, pb[:, 0:T].rearrange("p (b t) -> p b t", t=LS),
                       mod[:, l, gk * 16 + k, 1:17].unsqueeze(2).to_broadcast([128, NB, LS]), ALU.mult,
                       [pr, r_mod], [r_tmp[4]])
                    tt(xT[:, k, 0:T], xT[:, k, 0:T], tmp[4][:, 0:T], ALU.add, [r_xk[k], r_tmp[4]], [r_xk[k]])

            pending = []

            def flush_pending():
                while pending:
                    pending.pop(0)()

            def proj(l, col0, ncols, evac, extra_col0=None):
                ex = None if extra_col0 is None else w_in[l][:, extra_col0:extra_col0 + ncols]
                s = wload(w_in[l][:, col0:col0 + ncols], ncols, ex)
                n = ncols if ex is None else 2 * ncols
                for t in range(NT):
                    pb, pr = nb()
                    for k in range(16):
                        mm(pb[0:M, 0:n], hT[:, k, t * 128:t * 128 + M], wts[s][:, k, 0:n], k == 0, k == 15,
                           [r_hk[k], r_w[s]], [pr])
                    flush_pending()
                    evac(t, pb, pr)
                    yield

            def drive(fgs, bg=None, k=1):
                for fg in fgs:
                    for _ in fg:
                        for _i in range(k):
                            if bg is not None:
                                try:
                                    next(bg)
                                except StopIteration:
                                    bg = None
                flush_pending()
                if bg is not None:
                    for _ in bg:
                        pass

            rotA, rotB = tmp[0][:, 0:256], tmp[0][:, 256:512]
            rot_out = (tmp[2], tmp[1])
            r_rot_out = (r_tmp[2], r_tmp[1])
            bfbuf = (tmp[3][:, 0:256].bitcast(BF16), tmp[3][:, 256:512].bitcast(BF16))

            def rotary(t, pb, pr, out_f32, r_out):
                pv = pb[0:M, :].rearrange("p (h s d) -> p h s d", h=2, s=2)
                ov = out_f32[0:M, :].rearrange("p (h s d) -> p h s d", h=2, s=2)
                cosb = cst[0:M, 0, t, :].unsqueeze(1).to_broadcast([M, 2, 128])
                sinb = cst[0:M, 1, t, :].unsqueeze(1).to_broadcast([M, 2, 128])
                a_ = rotA[0:M, :].rearrange("p (h d) -> p h d", h=2)
                b_ = rotB[0:M, :].rearrange("p (h d) -> p h d", h=2)
                tt(a_, pv[:, :, 0, :], cosb, ALU.mult, [pr, r_cst], [r_rotA])
                tt(b_, pv[:, :, 1, :], sinb, ALU.mult, [pr, r_cst], [r_rotB])
                tt(ov[:, :, 0, :], a_, b_, ALU.subtract, [r_rotA, r_rotB], [r_out])
                tt(a_, pv[:, :, 0, :], sinb, ALU.mult, [pr, r_cst], [r_rotA])
                tt(b_, pv[:, :, 1, :], cosb, ALU.mult, [pr, r_cst], [r_rotB])
                tt(ov[:, :, 1, :], a_, b_, ALU.add, [r_rotA, r_rotB], [r_out])

            def transposes4(src_bf, r_src, dstT, r_dst, t, ncol=4):
                def go():
                    pb, pr = nb()
                    pbb = pb[:].bitcast(BF16)
                    for c in range(ncol):
                        tr(pbb[:, c * 128:c * 128 + M], src_bf[0:M, c * 128:(c + 1) * 128], identb[0:M, 0:M],
                           [r_src, r_const], [pr])
                    if ncol == 4:
                        act(dstT[:, :, t * 128:t * 128 + M], pbb[:, 0:512].rearrange("p (c m) -> p c m", c=4)[:, :, 0:M],
                            AF.Copy, [pr], [r_dst])
                    else:
                        act(dstT[:, 128 + t * 128:128 + t * 128 + M], pbb[:, 0:M], AF.Copy, [pr], [r_dst])
                pending.append(go)

            scale = 128.0 ** -0.5

            def layer(l):
                modulate(l, 0)

                def mk_ret_evacs(u):
                    def ev_q(t, pb, pr):
                        p_ = t % 2
                        rotary(t, pb, pr, rot_out[p_], r_rot_out[p_])
                        tt(bfbuf[p_][0:M, :].rearrange("p (h d) -> p h d", h=2),
                           rot_out[p_][0:M, :].rearrange("p (h d) -> p h d", h=2),
                           DEC[0:M, 0, 2 * u:2 * u + 2].unsqueeze(2).to_broadcast([M, 2, 256]), ALU.mult,
                           [r_rot_out[p_], r_const], [r_bf[p_]])
                        transposes4(bfbuf[p_], r_bf[p_], qdT, r_qdT, t)

                    def ev_k(t, pb, pr):
                        p_ = t % 2
                        rotary(t, pb, pr, rot_out[p_], r_rot_out[p_])
                        tt(bfbuf[p_][0:M, :].rearrange("p (h d) -> p h d", h=2),
                           rot_out[p_][0:M, :].rearrange("p (h d) -> p h d", h=2),
                           DEC[0:M, 1, 2 * u:2 * u + 2].unsqueeze(2).to_broadcast([M, 2, 256]), ALU.mult,
                           [r_rot_out[p_], r_const], [r_bf[p_]])
                        tt(kd[0:M, t, :].rearrange("p (h d) -> p h d", h=2),
                           rot_out[p_][0:M, :].rearrange("p (h d) -> p h d", h=2),
                           DEC[0:M, 2, 2 * u:2 * u + 2].unsqueeze(2).to_broadcast([M, 2, 256]), ALU.mult,
                           [r_rot_out[p_], r_const], [r_kd])
                        transposes4(bfbuf[p_], r_bf[p_], kdT, r_kdT, t)

                    def ev_v(t, pb, pr):
                        act(vv[0:M, t, :], pb[0:M, :], AF.Copy, [pr], [r_vv])

                    def ev_g(t, pb, pr):
                        act(tmp[4][0:M, :], pb[0:M, :], AF.Sigmoid, [pr], [r_tmp[4]])
                        tt(gm[0:M, t, :], tmp[4][0:M, :], pb[0:M, :], ALU.mult, [pr, r_tmp[4]], [r_gm])

                    def ev_m(t, pb, pr):
                        act(tmp[4][0:M, :], pb[0:M, :], AF.Sigmoid, [pr], [r_tmp[4]])
                        tt(gm[0:M, t, :], gm[0:M, t, :], tmp[4][0:M, :], ALU.mult, [r_gm, r_tmp[4]], [r_gm])
                    return ev_q, ev_k, ev_v, ev_g, ev_m

                def ret_norm_gate(po, pro, t, u, hh):
                    act(tmp[5][0:M, 0:256], po[0:M, 0:256], AF.Square, [pro], [r_tmp[5], r_sm[0]], accum_out=sm[0][0:M, 0:1])
                    rstd_from(sm[2][0:M, 0:1], sm[0][0:M, 0:1], 256.0, [r_sm[0], r_const], [r_sm[2]], sm[1][0:M, 0:1], r_sm[1])
                    stt(mg[0:M, t, u * 512 + hh * 256: u * 512 + (hh + 1) * 256], po[0:M, 0:256], sm[2][0:M, 0:1],
                        gm[0:M, t, hh * 256:(hh + 1) * 256], ALU.mult, ALU.mult, [pro, r_sm[2], r_gm], [r_mg])

                def ret_sweep_prompt(u):
                    for hh in range(2):
                        H = 2 * u + hh
                        if first:
                            P.op("dve", lambda E: E.memset(S[:].rearrange("p c v -> p (c v)"), 0.0), [], [r_S])
                        else:
                            dma("sp", S, st_p[l, H].rearrange("(c p) v -> p c v", p=128), [rd("st_p")], [r_S], r_S)
                        cp(Sb, S, [r_S], [r_Sb], eng="act")
                        att = {}

                        def emit_att(t):
                            tsl = slice(t * 128, (t + 1) * 128)
                            pa, pra = nb()
                            for c in range(2):
                                mm(pa[:, 0:128], kdT[:, 2 * hh + c, tsl], qdT[:, 2 * hh + c, tsl], c == 0, c == 1,
                                   [r_kdT, r_qdT], [pra])
                            i = t % 2
                            tt(attb[i], pa[:, 0:128], mk[0], ALU.mult, [pra, r_const], [r_attb[i]])
                        emit_att(0)
                        yield
                        for t in range(NT):
                            tsl = slice(t * 128, (t + 1) * 128)
                            i = t % 2
                            po, pro = nb()
                            mm(po[:, 0:256], attb[i], vv[:, t, hh * 256:(hh + 1) * 256], True, False, [r_attb[i], r_vv], [pro])
                            for c in range(2):
                                mm(po[:, 0:256], qdT[:, 2 * hh + c, tsl], Sb[:, c, :], False, c == 1, [r_qdT, r_Sb], [pro])
                            ret_norm_gate(po, pro, t, u, hh)
                            for c in range(2):
                                pu, pru = nb()
                                mm(pu[:, 0:256], kd[:, t, hh * 256 + c * 128: hh * 256 + (c + 1) * 128],
                                   vv[:, t, hh * 256:(hh + 1) * 256], True, True, [r_kd, r_vv], [pru])
                                stt(S[:, c, :], S[:, c, :], DEC[:, 3, H:H + 1], pu[:, 0:256], ALU.mult, ALU.add,
                                    [r_S, pru, r_const], [r_S])
                            if t + 1 < NT:
                                cp(Sb, S, [r_S], [r_Sb], eng="act")
                                emit_att(t + 1)
                            yield
                        dma("sp", st_p[l, H].rearrange("(c p) v -> p c v", p=128), S, [r_S], [rd("st_p")], r_S)

                def ret_sweep_sample(u):
                    for hh in range(2):
                        H = 2 * u + hh
                        pa, pra = nb()
                        for c in range(2):
                            mm(pa[0:M, 0:M], kdT[:, 2 * hh + c, 0:M], qdT[:, 2 * hh + c, 0:M], c == 0, c == 1,
                               [r_kdT, r_qdT], [pra])
                        tt(attb[0][0:M, 0:M], pa[0:M, 0:M], mk[2][0:M, 0:M], ALU.mult, [pra, r_const], [r_attb[0]])
                        po, pro = psb[7], psr[7]
                        mm(po[0:M, 0:256], attb[0][0:M, 0:M], vv[0:M, 0, hh * 256:(hh + 1) * 256], True, False,
                           [r_attb[0], r_vv], [pro])
                        KS = 6

                        def ld(b):
                            j = b % KS
                            dma("act", s0[j], st_in[l, b, H].rearrange("(c p) v -> p c v", p=128), [r_qdT], [r_s0[j]], r_s0[j])
                        for b in range(KS - 1):
                            ld(b)
                        for b in range(NB):
                            if b + KS - 1 < NB:
                                ld(b + KS - 1)
                            j = b % KS
                            i = b % 2
                            cp(s0b[j], s0[j], [r_s0[j]], [r_s0b[j]], eng="act")
                            tt(qexp[i], qdT[:, 2 * hh:2 * hh + 2, 0:64], bmTb[:, b, :].unsqueeze(1).to_broadcast([128, 2, 64]),
                               ALU.mult, [r_qdT, r_const], [r_qexp[i]])
                            ts(kexp[i][0:64, :], kd[0:64, 0, hh * 256:(hh + 1) * 256], bm[0:64, b:b + 1], None, ALU.mult, None,
                               [r_kd, r_const], [r_kexp[i]])
                            for c in range(2):
                                mm(po[0:M, 0:256], qexp[i][:, c, :], s0b[j][:, c, :], False,
                                   b == NB - 1 and c == 1, [r_qexp[i], r_s0b[j]], [pro])
                            for c in range(2):
                                pu, pru = nb()
                                mm(pu[:, 0:256], kexp[i][0:64, c * 128:(c + 1) * 128], vv[0:64, 0, hh * 256:(hh + 1) * 256],
                                   True, True, [r_kexp[i], r_vv], [pru])
                                stt(s0[j][:, c, :], s0[j][:, c, :], DEC[:, 3, H:H + 1], pu[:, 0:256], ALU.mult, ALU.add,
                                    [r_s0[j], pru, r_const], [r_s0[j]])
                            dma("sp", st_s[l, b, H].rearrange("(c p) v -> p c v", p=128), s0[j], [r_s0[j]], [rd("st_s")], r_s0[j])
                            if b % 2 == 1 and b < NB - 1:
                                yield
                        ret_norm_gate(po, pro, 0, u, hh)
                    yield

                def mk_swa_evacs(u):
                    def ev_qa(t, pb, pr):
                        p_ = t % 2
                        sqb, r_sqb = rot_out[p_], r_rot_out[p_]
                        act(sqb[0:M, :], pb[0:M, :], AF.Square, [pr], [r_sqb])
                        P.op("dve", lambda E: E.tensor_reduce(out=sm[3][0:M, 0:4], in_=sqb[0:M, :].rearrange("p (g d) -> p g d", g=4),
                                                              axis=AX.X, op=ALU.add), [r_sqb], [r_sm[3]])
                        rstd_from(sm[5][0:M, 0:4], sm[3][0:M, 0:4], 128.0, [r_sm[3], r_const], [r_sm[5]], sm[4][0:M, 0:4], r_sm[4])
                        tt(sqb[0:M, :].rearrange("p (g d) -> p g d", g=4), pb[0:M, :].rearrange("p (g d) -> p g d", g=4),
                           sm[5][0:M, 0:4].unsqueeze(2).to_broadcast([M, 4, 128]), ALU.mult, [pr, r_sm[5]], [r_sqb])
                        tt(bfbuf[p_][0:M, :].rearrange("p (g d) -> p g d", g=4),
                           sqb[0:M, :].rearrange("p (g d) -> p g d", g=4),
                           qkw[0:M, l, 0, :].unsqueeze(1).to_broadcast([M, 4, 128]), ALU.mult, [r_sqb, r_const], [r_bf[p_]])
                        transposes4(bfbuf[p_], r_bf[p_], qaT, r_qaT, t)

                    def ev_kv(t, pb, pr):
                        p_ = t % 2
                        sqb, r_sqb = rot_out[p_], r_rot_out[p_]
                        act(sqb[0:M, 0:128], pb[0:M, 0:128], AF.Square, [pr], [r_sqb, r_sm[3]], accum_out=sm[3][0:M, 0:1])
                        rstd_from(sm[5][0:M, 0:1], sm[3][0:M, 0:1], 128.0, [r_sm[3], r_const], [r_sm[5]], sm[4][0:M, 0:1], r_sm[4])
                        stt(kvst[0:M, 0:128], pb[0:M, 0:128], sm[5][0:M, 0:1], qkw[0:M, l, 1, :], ALU.mult, ALU.mult,
                            [pr, r_sm[5], r_const], [r_kvst])
                        act(kvst[0:M, 128:256], pb[0:M, 128:256], AF.Copy, [pr], [r_kvst])
                        cp(bfbuf[p_][0:M, 0:128], kvst[0:M, 0:128], [r_kvst], [r_bf[p_]])
                        cp(vext[0:M, t + 1, 0:128], kvst[0:M, 128:256], [r_kvst], [r_vext])
                        transposes4(bfbuf[p_], r_bf[p_], kT, r_kT, t, ncol=1)
                        if samp:
                            for b in range(NB):
                                dma("pool", cks[l, b, 124:128, u, :], kvst[b * 4:(b + 1) * 4, 0:128], [r_kvst], [rd("cks")], r_kvst)
                                dma("pool", cvs[l, b, 124:128, u, :], kvst[b * 4:(b + 1) * 4, 128:256], [r_kvst], [rd("cvs")], r_kvst)
                        elif last and t == NT - 1:
                            dma("sp", ckp[l, :, u, :], kvst[:, 0:128], [r_kvst], [rd("ckp")], r_kvst)
                            dma("sp", cvp[l, :, u, :], kvst[:, 128:256], [r_kvst], [rd("cvp")], r_kvst)

                    def ev_ma(t, pb, pr):
                        act(sgm[0:M, t, :], pb[0:M, :], AF.Sigmoid, [pr], [r_sgm])
                    return ev_qa, ev_kv, ev_ma

                def swa_finish(po, pro, t, u, g2, Mq):
                    pov = po[0:Mq, 0:260].rearrange("p (g d) -> p g d", g=2)
                    tt(sm[0][0:Mq, 0:2], pov[:, :, 128], esink[0:Mq, l, 4 * u + 2 * g2: 4 * u + 2 * g2 + 2], ALU.add,
                       [pro, r_const], [r_sm[0]])
                    recip(sm[1][0:Mq, 0:2], sm[0][0:Mq, 0:2], [r_sm[0]], [r_sm[1]])
                    for gg in range(2):
                        g = g2 * 2 + gg
                        stt(tmp[5][0:Mq, gg * 128:(gg + 1) * 128], po[0:Mq, gg * 130:gg * 130 + 128], sm[1][0:Mq, gg:gg + 1],
                            sgm[0:Mq, t, g * 128:(g + 1) * 128], ALU.mult, ALU.mult, [pro, r_sm[1], r_sgm], [r_tmp[5]])
                    c0 = u * 512 + g2 * 256
                    tt(mg[0:Mq, t, c0:c0 + 256], mg[0:Mq, t, c0:c0 + 256], tmp[5][0:Mq, 0:256], ALU.add, [r_mg, r_tmp[5]], [r_mg])

                def swa_sweep_prompt(u):
                    if not first:
                        cp(kT[:, 0:128], kprev[:, l, u, :], [r_carry], [r_kT])
                        cp(vext[:, 0, :], vprev[:, l, u, :], [r_carry], [r_vext])
                    cur = {}

                    def emit_scores(t):
                        tsl = slice(t * 128, (t + 1) * 128)
                        has_prev = not (first and t == 0)
                        blocks = [(1, mk[0])] + ([(0, mk[1])] if has_prev else [])
                        pbs = []
                        for bi, (off_t, msk) in enumerate(blocks):
                            j = (t % 2) * 2 + bi
                            pscore, prs = nb()
                            mm(pscore[:, :], kT[:, (t + off_t) * 128:(t + off_t + 1) * 128], qaT[:, :, tsl], True, True,
                               [r_kT, r_qaT], [prs])
                            act(pbuf[j], pscore[:, :], AF.Exp, [prs], [r_pbuf[j]], scale=scale)
                            tt(pbuf[j].rearrange("p (g q) -> p g q", g=4), pbuf[j].rearrange("p (g q) -> p g q", g=4),
                               msk.unsqueeze(1).to_broadcast([128, 4, 128]), ALU.mult, [r_pbuf[j], r_const], [r_pbuf[j]])
                            pbs.append((j, t + off_t))
                        cur[t] = pbs
                    emit_scores(0)
                    yield
                    for t in range(NT):
                        pbs = cur.pop(t)
                        for g2 in range(2):
                            po, pro = nb()
                            for gg in range(2):
                                g = g2 * 2 + gg
                                for n_, (j, vt) in enumerate(pbs):
                                    mm(po[:, gg * 130:gg * 130 + 129], pbuf[j][:, g * 128:(g + 1) * 128], vext[:, vt, 0:129],
                                       n_ == 0, n_ == len(pbs) - 1, [r_pbuf[j], r_vext], [pro])
                            swa_finish(po, pro, t, u, g2, 128)
                        if t + 1 < NT:
                            emit_scores(t + 1)
                        yield
                    cp(kprev[:, l, u, :], kT[:, TP:TP + 128], [r_kT], [r_carry])
                    cp(vprev[:, l, u, :], vext[:, NT, :], [r_vext], [r_carry])

                def swa_sweep_sample(u):
                    pss, prss = nb()
                    mm(pss[0:64, 0:256], kT[:, 128:192], qaT[:, :, 0:64], True, True, [r_kT, r_qaT], [prss])
                    act(pbuf[0][0:64, 0:256], pss[0:64, 0:256], AF.Exp, [prss], [r_pbuf[0]], scale=scale)
                    tt(pbuf[0][0:64, 0:256].rearrange("p (g q) -> p g q", g=4), pbuf[0][0:64, 0:256].rearrange("p (g q) -> p g q", g=4),
                       mk[2][0:64, 0:64].unsqueeze(1).to_broadcast([64, 4, 64]), ALU.mult, [r_pbuf[0], r_const], [r_pbuf[0]])
                    psp, prsp = psb[7], psr[7]
                    P.op("dve", lambda E: E.memset(vcall[:, :, 128:130], 1.0), [], [r_vcall])
                    KS = 4

                    def ldc(b):
                        i = b % KS
                        dma("sp", cst_k[i], ck_in[l, b, :, u, :], [], [r_cstk[i]], r_cstk[i])
                        dma("sp", cst_v[i], cv_in[l, b, :, u, :], [], [r_cstv[i]], r_cstv[i])
                    for b in range(KS - 1):
                        ldc(b)
                    for b in range(NB):
                        if b + KS - 1 < NB:
                            ldc(b + KS - 1)
                        i = b % KS
                        cp(kcb[i], cst_k[i], [r_cstk[i]], [r_kcb[i]])
                        cp(vcall[:, b, 0:128], cst_v[i], [r_cstv[i]], [r_vcall], eng="act")
                        pt_, prt = nb()
                        tr(pt_[:].bitcast(BF16)[:, 0:128], kcb[i], identb, [r_kcb[i], r_const], [prt])
                        act(kcT[i], pt_[:].bitcast(BF16)[:, 0:128], AF.Copy, [prt], [r_kcT[i]])
                        mm(psp[:, 0:256].rearrange("p (g q) -> p g q", g=4)[:, :, b * 4:(b + 1) * 4], kcT[i],
                           qaT[:, :, b * 4:(b + 1) * 4], True, True, [r_kcT[i], r_qaT], [prsp])
                        dma("pool", cks[l, b, 0:124, u, :], cst_k[i][4:128, :], [r_cstk[i]], [rd("cks")], r_cstk[i])
                        dma("pool", cvs[l, b, 0:124, u, :], cst_v[i][4:128, :], [r_cstv[i]], [rd("cvs")], r_cstv[i])
                        if b % 2 == 1:
                            yield
                    act(pbuf[1][:, 0:256], psp[:, 0:256], AF.Exp, [prsp], [r_pbuf[1]], scale=scale)
                    tt(pbuf[1][:, 0:256].rearrange("p (g q) -> p g q", g=4), pbuf[1][:, 0:256].rearrange("p (g q) -> p g q", g=4),
                       mk[3][:, 0:64].unsqueeze(1).to_broadcast([128, 4, 64]), ALU.mult, [r_pbuf[1], r_const], [r_pbuf[1]])
                    for g2 in range(2):
                        po, pro = nb()
                        for gg in range(2):
                            g = g2 * 2 + gg
                            for b in range(NB):
                                i = b % 2
                                tt(pexp[i], pbuf[1][:, g * 64:(g + 1) * 64], bmTb[:, b, :], ALU.mult,
                                   [r_pbuf[1], r_const], [r_pexp[i]])
                                mm(po[0:64, gg * 130:gg * 130 + 129], pexp[i], vcall[:, b, 0:129], b == 0, False,
                                   [r_pexp[i], r_vcall], [pro])
                            mm(po[0:64, gg * 130:gg * 130 + 129], pbuf[0][0:64, g * 64:(g + 1) * 64], vext[0:64, 1, 0:129],
                               False, True, [r_pbuf[0], r_vext], [pro])
                        swa_finish(po, pro, 0, u, g2, 64)
                    yield

                ret_sweep = ret_sweep_sample if samp else ret_sweep_prompt
                swa_sweep = swa_sweep_sample if samp else swa_sweep_prompt
                bg = None
                for u in range(4):
                    ev_q, ev_k, ev_v, ev_g, ev_m = mk_ret_evacs(u)
                    drive([proj(l, OQ + u * 512, 512, ev_q), proj(l, OK_ + u * 512, 512, ev_k),
                           proj(l, OV + u * 512, 512, ev_v), proj(l, OG + u * 512, 512, ev_g),
                           proj(l, OMR + u * 512, 512, ev_m)], bg, k=(3 if samp else 1))
                    bg = ret_sweep(u)
                    ev_qa, ev_kv, ev_ma = mk_swa_evacs(u)
                    P.op("dve", lambda E: E.memset(vext[:, :, 128:130], 1.0), [], [r_vext])
                    drive([proj(l, OQA + u * 512, 512, ev_qa),
                           proj(l, OKA + u * 128, 128, ev_kv, extra_col0=OVA + u * 128),
                           proj(l, OMA + u * 512, 512, ev_ma)], bg, k=(6 if samp else 1))
                    bg = swa_sweep(u)
                drive([], bg)

                for t in range(NT):
                    for kb in range(2):
                        pb, pr = nb()
                        pbb = pb[:].bitcast(BF16)
                        for k8 in range(8):
                            k = kb * 8 + k8
                            tr(pbb[:, k8 * 128:k8 * 128 + M], mg[0:M, t, k * 128:(k + 1) * 128], identb[0:M, 0:M],
                               [r_mg, r_const], [pr])
                        act(hT[:, kb * 8:(kb + 1) * 8, t * 128:t * 128 + M],
                            pbb[:, :].rearrange("p (k m) -> p k m", k=8)[:, :, 0:M], AF.Copy, [pr], r_hk[kb * 8:(kb + 1) * 8])
                for ct in range(4):
                    s = wload(w_out[l][:, ct * 512:(ct + 1) * 512], 512)
                    for c4 in range(4):
                        pb, pr = nb()
                        for k in range(16):
                            mm(pb[:, 0:T], wts[s][:, k, c4 * 128:(c4 + 1) * 128], hT[:, k, 0:T], k == 0, k == 15,
                               [r_w[s], r_hk[k]], [pr])
                        resid_add(l, 2, ct * 4 + c4, pb, pr)
                modulate(l, 1)
                for fg in range(16):
                    s = wload(w_up[l][:, fg * 512:(fg + 1) * 512], 512)
                    ui = fg % 2
                    for c4 in range(4):
                        pb, pr = nb()
                        for k in range(16):
                            mm(pb[:, 0:T], wts[s][:, k, c4 * 128:(c4 + 1) * 128], hT[:, k, 0:T], k == 0, k == 15,
                               [r_w[s], r_hk[k]], [pr])
                        act(tmp[4][:, 0:T], pb[:, 0:T], AF.Relu, [pr], [r_tmp[4]])
                        tt(ug[ui][:, c4, 0:T], tmp[4][:, 0:T], tmp[4][:, 0:T], ALU.mult, [r_tmp[4]], [r_ug[ui]])
                    s2 = wload(w_down[l][fg * 512:(fg + 1) * 512, :], 2048, down=True)
                    wdv = wts[s2][:].rearrange("p k c -> p (k c)").rearrange("p (f c) -> p f c", f=4)
                    for dc in range(16):
                        pb, pr = nb()
                        for f in range(4):
                            mm(pb[:, 0:T], wdv[:, f, dc * 128:(dc + 1) * 128], ug[ui][:, f, 0:T], f == 0, f == 3,
                               [r_w[s2], r_ug[ui]], [pr])
                        resid_add(l, 5, dc, pb, pr)

            for l in range(DEPTH):
                layer(l)

            for t in range(NT):
                for kb in range(4):
                    pb, pr = nb()
                    for k4 in range(4):
                        k = kb * 4 + k4
                        tr(pb[0:M, k4 * 128:(k4 + 1) * 128], xT[:, k, t * 128:t * 128 + M], ident, [r_xk[k], r_const], [pr])
                    act(xin[0:M, kb * 512:(kb + 1) * 512], pb[0:M, :], AF.Copy, [pr], [r_xin])
                dma("sp", ydst[tok0 + t * 128: tok0 + t * 128 + M, :], xin[0:M, :], [r_xin], [rd("y")], r_xin)

        for p_ in range(SEQ // TP):
            wmode[0] = "p0" if p_ == 0 else ("p1" if p_ == 1 else "read")
            wtid[0] = 0
            run_pass("p", p_ * TP)
            if p_ <= 1:
                flush_wb()
                P.barrier()
        P.barrier()
        wmode[0] = "read"
        wtid[0] = 0
        run_pass("s", 0)
        P.barrier()
        P.emit()
        print("stats", P.stats)
    return nc


_CACHE = {}


def kernel(x_prompt, x_sample, c_prompt, c_sample, state_ret, cache_k_win, cache_v_win,
           norm1_w, norm2_w, w_ada, b_ada, w_in, q_norm_w, k_norm_w, sinks, w_out, w_up, w_down):
    f = lambda a: np.ascontiguousarray(np.asarray(a, dtype=np.float32))
    if "nc" not in _CACHE:
        _CACHE["nc"] = build_nc()
        _CACHE["consts"] = _consts()
    nc = _CACHE["nc"]
    cst = _CACHE["consts"]
    x_prompt, x_sample, c_prompt, c_sample = f(x_prompt), f(x_sample), f(c_prompt), f(c_sample)
    state_ret, cache_k_win, cache_v_win = f(state_ret), f(cache_k_win), f(cache_v_win)
    shared = {
        "n1w": f(norm1_w).reshape(DEPTH, 16, 128), "n2w": f(norm2_w).reshape(DEPTH, 16, 128),
        "w_ada": f(w_ada), "b_ada": f(b_ada).reshape(DEPTH, 96, 128), "w_in": f(w_in),
        "qnw": f(q_norm_w), "knw": f(k_norm_w), "sinks": f(sinks),
        "w_out": f(w_out), "w_up": f(w_up), "w_down": f(w_down),
        "cs_p": cst["cs_p"], "cs_s": cst["cs_s"], "dec": cst["dec"], "decs": cst["decs"],
        "masks": cst["masks"], "bm": cst["bm"], "bmT": cst["bmT"],
    }
    in_maps = []
    for c in range(8):
        m = dict(shared)
        m["xp"] = x_prompt[c]
        m["xs"] = x_sample[NB * c:NB * (c + 1)].reshape(NB * LS, D)
        m["cv"] = np.concatenate([c_prompt[c:c + 1], c_sample[NB * c:NB * (c + 1)]], 0)
        m["st_in"] = np.ascontiguousarray(state_ret[:, NB * c:NB * (c + 1)])
        m["ck_in"] = np.ascontiguousarray(cache_k_win[:, NB * c:NB * (c + 1)])
        m["cv_in"] = np.ascontiguousarray(cache_v_win[:, NB * c:NB * (c + 1)])
        in_maps.append(m)
    res = run_bass_kernel_spmd(nc, in_maps, core_ids=list(range(8)))
    R = res.results
    y_p = np.stack([R[c]["yp"] for c in range(8)], 0)
    y_s = np.concatenate([R[c]["ys"].reshape(NB, LS, D) for c in range(8)], 0)
    st_p = np.stack([R[c]["st_p"] for c in range(8)], 1)
    ck_p = np.stack([R[c]["ckp"] for c in range(8)], 1)
    cv_p = np.stack([R[c]["cvp"] for c in range(8)], 1)
    st_s = np.concatenate([R[c]["st_s"] for c in range(8)], 1)
    ck_s = np.concatenate([R[c]["cks"] for c in range(8)], 1)
    cv_s = np.concatenate([R[c]["cvs"] for c in range(8)], 1)
    return (y_p, y_s, st_p, ck_p, cv_p, st_s, ck_s, cv_s)
```

```python
from contextlib import ExitStack
import numpy as np
import concourse.bass as bass
import concourse.mybir as mybir
from concourse.bass_utils import run_bass_kernel_spmd

F32 = mybir.dt.float32
BF16 = mybir.dt.bfloat16
ALU = mybir.AluOpType
AF = mybir.ActivationFunctionType
AX = mybir.AxisListType

D = 2048
SEQ = 2048
NB = 16
LS = 4
PAST = 8192
DEPTH = 2
EPS = 1e-6
TP = 512
OQ, OK_, OV, OG, OQA, OKA, OVA, OMR, OMA = 0, 2048, 4096, 6144, 8192, 10240, 10752, 11264, 13312
INW = 15360


class Res:
    __slots__ = ("name", "w", "r", "sem", "cnt")

    def __init__(self, name):
        self.name = name
        self.w = None
        self.r = {}
        self.sem = None
        self.cnt = 0


class Op:
    __slots__ = ("eng", "fn", "deps", "idx", "mark", "dma", "markval")

    def __init__(self, eng, fn):
        self.eng = eng
        self.fn = fn
        self.deps = []
        self.mark = False
        self.dma = None
        self.markval = 0


class Prog:
    ENGS = ("pe", "act", "dve", "pool", "sp")

    def __init__(self, nc, stack):
        self.nc = nc
        self.stack = stack
        self.ops = {e: [] for e in self.ENGS}
        self.known = {e: {} for e in self.ENGS}
        self.esem = {e: stack.enter_context(nc.semaphore("s_" + e)) for e in self.ENGS}
        self.nres = 0
        self.allres = []

    def res(self, name=None):
        self.nres += 1
        r = Res(name or "r%d" % self.nres)
        self.allres.append(r)
        return r

    def _apply(self, op, need):
        eng = op.eng
        kn = self.known[eng]
        for key, val in need.items():
            if isinstance(key, str):
                if key == eng:
                    if eng == "pe" or len(self.ops[eng]) - val > 2:
                        continue
                if kn.get(key, -1) >= val:
                    continue
                kn[key] = val
                self.ops[key][val].mark = True
                op.deps.append(("e", key, val))
            else:
                if kn.get(key, -1) >= val:
                    continue
                kn[key] = val
                op.deps.append(("d", key, val))

    @staticmethod
    def _acc(need, tok):
        if tok is None:
            return
        k, v = tok
        if need.get(k, -1) < v:
            need[k] = v

    def op(self, eng, fn, reads=(), writes=(), dma_key=None):
        o = Op(eng, fn)
        o.idx = len(self.ops[eng])
        need = {}
        for r in reads:
            self._acc(need, r.w)
        for r in writes:
            self._acc(need, r.w)
            for k, v in r.r.items():
                self._acc(need, (k, v))
        self._apply(o, need)
        if dma_key is not None:
            if dma_key.sem is None:
                dma_key.sem = self.stack.enter_context(self.nc.semaphore("d_" + dma_key.name))
            dma_key.cnt += 1
            o.dma = dma_key
            tok = (dma_key, dma_key.cnt * 16)
        else:
            tok = (eng, o.idx)
        for r in reads:
            if r.r.get(tok[0], -1) < tok[1]:
                r.r[tok[0]] = tok[1]
        for r in writes:
            r.w = tok
            r.r = {}
        self.ops[eng].append(o)
        return o

    def barrier(self):
        last = {}
        for e in self.ENGS:
            if self.ops[e]:
                for o in reversed(self.ops[e]):
                    if o.fn is not None and o.dma is None:
                        last[e] = o.idx
                        break
        for e in self.ENGS:
            o = Op(e, None)
            o.idx = len(self.ops[e])
            need = {k: v for k, v in last.items() if k != e}
            for r in self.allres:
                if r.sem is not None and r.cnt:
                    need[r] = r.cnt * 16
            self._apply(o, need)
            self.ops[e].append(o)

    def emit(self):
        nc = self.nc
        for e in self.ENGS:
            c = 0
            for o in self.ops[e]:
                if o.mark:
                    c += 1
                    o.markval = c
        self.stats = {e: (len(self.ops[e]), sum(1 for o in self.ops[e] if o.mark),
                          sum(len(o.deps) for o in self.ops[e])) for e in self.ENGS}

        def run(e, E):
            for o in self.ops[e]:
                for tok in o.deps:
                    if tok[0] == "e":
                        E.wait_ge(self.esem[tok[1]], self.ops[tok[1]][tok[2]].markval)
                    else:
                        E.wait_ge(tok[1].sem, tok[2])
                if o.fn is None:
                    continue
                ins = o.fn(E)
                if o.dma is not None:
                    ins.then_inc(o.dma.sem, 16)
                elif o.mark:
                    ins.then_inc(self.esem[e], 1)

        with nc.Block() as block:
            @block.tensor
            def _(E):
                run("pe", E)

            @block.scalar
            def _(E):
                run("act", E)

            @block.vector
            def _(E):
                run("dve", E)

            @block.gpsimd
            def _(E):
                run("pool", E)

            @block.sync
            def _(E):
                run("sp", E)


def _consts():
    half = 128
    inv = (1.0 / (10000.0 ** (np.arange(half, dtype=np.float32) / np.float32(half)))).astype(np.float32)
    pos_p = np.arange(SEQ, dtype=np.float32)
    ang = (pos_p[:, None] * inv[None, :]).astype(np.float32)
    cs_p = np.stack([np.cos(ang), np.sin(ang)], 0).astype(np.float32)
    pos_s = (PAST + (np.arange(NB * LS) % LS)).astype(np.float32)
    ang = (pos_s[:, None] * inv[None, :]).astype(np.float32)
    cs_s = np.stack([np.cos(ang), np.sin(ang)], 0).astype(np.float32)
    h = np.arange(8, dtype=np.float64)
    log_g = np.log1p(-np.exp2(-5.0 - h))
    i = np.arange(128, dtype=np.float64)
    dec = np.zeros((4, 128, 8), np.float32)
    dec[0] = np.exp((i[:, None] + 1.0) * log_g[None, :])
    dec[1] = np.exp(-(i[:, None] + 1.0) * log_g[None, :]) / 16.0
    dec[2] = np.exp((127.0 - i[:, None]) * log_g[None, :]) / 16.0
    dec[3] = np.exp(128.0 * log_g)[None, :]
    t = (np.arange(128) % LS).astype(np.float64)
    decs = np.zeros((4, 128, 8), np.float32)
    decs[0] = np.exp((t[:, None] + 1.0) * log_g[None, :])
    decs[1] = np.exp(-(t[:, None] + 1.0) * log_g[None, :]) / 16.0
    decs[2] = np.exp((LS - 1.0 - t[:, None]) * log_g[None, :]) / 16.0
    decs[3] = np.exp(float(LS) * log_g)[None, :]
    j = np.arange(128)
    m = np.zeros((6, 128, 128), np.float32)
    m[0] = (j[:, None] <= j[None, :])
    m[1] = (j[:, None] >= j[None, :])
    bb = j // LS
    tt = j % LS
    m[2] = (bb[:, None] == bb[None, :]) & (tt[:, None] <= tt[None, :])
    m[3] = (j[:, None] >= tt[None, :])
    m[4] = np.eye(128)
    m[5] = 1.0
    bm = np.zeros((128, 16), np.float32)
    bm[:64] = (bb[:64, None] == np.arange(16)[None, :])
    bmT = np.zeros((128, 16, 64), np.float32)
    bmT[:] = (np.arange(16)[:, None] == bb[None, :64])[None]
    return dict(cs_p=cs_p, cs_s=cs_s, dec=dec, decs=decs, masks=m, bm=bm, bmT=bmT)


def build_nc():
    nc = bass.Bass("TRN2", target_bir_lowering=False)

    def din(name, shape, dt=F32):
        return nc.dram_tensor(name, list(shape), dt, kind="ExternalInput").ap()

    def dout(name, shape):
        return nc.dram_tensor(name, list(shape), F32, kind="ExternalOutput").ap()

    xp = din("xp", [SEQ, D])
    xs = din("xs", [NB * LS, D])
    cv = din("cv", [NB + 1, D])
    st_in = din("st_in", [DEPTH, NB, 8, 256, 256])
    ck_in = din("ck_in", [DEPTH, NB, 128, 4, 128])
    cv_in = din("cv_in", [DEPTH, NB, 128, 4, 128])
    n1w = din("n1w", [DEPTH, 16, 128])
    n2w = din("n2w", [DEPTH, 16, 128])
    w_ada = din("w_ada", [DEPTH, D, 6 * D])
    b_ada = din("b_ada", [DEPTH, 96, 128])
    w_in = din("w_in", [DEPTH, D, INW])
    qnw = din("qnw", [DEPTH, 128])
    knw = din("knw", [DEPTH, 128])
    sinks = din("sinks", [DEPTH, 16])
    w_out = din("w_out", [DEPTH, D, D])
    w_up = din("w_up", [DEPTH, D, 4 * D])
    w_down = din("w_down", [DEPTH, 4 * D, D])
    cs_p = din("cs_p", [2, SEQ, 128])
    cs_s = din("cs_s", [2, NB * LS, 128])
    dec_d = din("dec", [4, 128, 8])
    decs_d = din("decs", [4, 128, 8])
    masks_d = din("masks", [6, 128, 128])
    bm_d = din("bm", [128, 16])
    bmT_d = din("bmT", [128, 16, 64])

    wbfL = [nc.dram_tensor("wbf%d" % i, [68, 128, 8192], BF16, kind="Internal").ap() for i in range(2)]

    class _W:
        def __getitem__(self, tid):
            return wbfL[tid // 68][tid % 68]
    wbf = _W()
    yp = dout("yp", [SEQ, D])
    ys = dout("ys", [NB * LS, D])
    st_p = dout("st_p", [DEPTH, 8, 256, 256])
    ckp = dout("ckp", [DEPTH, 128, 4, 128])
    cvp = dout("cvp", [DEPTH, 128, 4, 128])
    st_s = dout("st_s", [DEPTH, NB, 8, 256, 256])
    cks = dout("cks", [DEPTH, NB, 128, 4, 128])
    cvs = dout("cvs", [DEPTH, NB, 128, 4, 128])

    with ExitStack() as st:
        P = Prog(nc, st)
        AW = 52000
        arena = st.enter_context(nc.sbuf_tensor("arena", [128, AW], F32))
        psb = [st.enter_context(nc.psum_tensor("ps%d" % i, [128, 512], F32)) for i in range(8)]
        psr = [P.res("ps%d" % i) for i in range(8)]
        pctr = [0]

        def nb():
            i = pctr[0] % 7
            pctr[0] += 1
            return psb[i], psr[i]

        off = [0]

        def carve(nwords, dt=F32):
            a = off[0]
            off[0] += nwords
            assert off[0] <= AW, off[0]
            v = arena[:, a:a + nwords]
            return v.bitcast(BF16) if dt == BF16 else v

        def mm(out, lhsT, rhs, start, stop, reads, writes):
            P.op("pe", lambda E: E.matmul(out, lhsT=lhsT, rhs=rhs, start=start, stop=stop), reads, writes)

        def tr(out, in_, ident, reads, writes):
            P.op("pe", lambda E: E.transpose(out=out, in_=in_, identity=ident), reads, writes)

        def act(out, in_, func, reads, writes, **kw):
            P.op("act", lambda E: E.activation(out=out, in_=in_, func=func, **kw), reads, writes)

        def tt(out, in0, in1, op, reads, writes, eng="dve"):
            P.op(eng, lambda E: E.tensor_tensor(out=out, in0=in0, in1=in1, op=op), reads, writes)

        def ts(out, in0, s1, s2, op0, op1, reads, writes, eng="dve"):
            if op1 is None:
                P.op(eng, lambda E: E.tensor_scalar(out=out, in0=in0, scalar1=s1, scalar2=None, op0=op0), reads, writes)
            else:
                P.op(eng, lambda E: E.tensor_scalar(out=out, in0=in0, scalar1=s1, scalar2=s2, op0=op0, op1=op1), reads, writes)

        def stt(out, in0, scalar, in1, op0, op1, reads, writes):
            P.op("dve", lambda E: E.scalar_tensor_tensor(out=out, in0=in0, scalar=scalar, in1=in1, op0=op0, op1=op1), reads, writes)

        def cp(out, in_, reads, writes, eng="dve"):
            if eng == "act":
                P.op("act", lambda E: E.copy(out=out, in_=in_), reads, writes)
            else:
                P.op(eng, lambda E: E.tensor_copy(out=out, in_=in_), reads, writes)

        def recip(out, in_, reads, writes):
            P.op("dve", lambda E: E.reciprocal(out=out, in_=in_), reads, writes)

        def dma(q, out, in_, reads, writes, key):
            P.op(q, lambda E: E.dma_start(out=out, in_=in_), reads, writes, dma_key=key)

        def rstd_from(out, in_, n, reads, writes, tmp, r_tmp):
            act(tmp, in_, AF.Sqrt, reads, [r_tmp], bias=epsc[:in_.shape[0], :], scale=1.0 / n)
            recip(out, tmp, [r_tmp], writes)

        ident = carve(128)
        identb = carve(64, BF16)
        onesb = carve(64, BF16)
        mk = [carve(64, BF16) for _ in range(4)]
        mstage = carve(128)
        epsc = carve(1)
        dec = carve(32).rearrange("p (a h) -> p a h", a=4)
        decs = carve(32).rearrange("p (a h) -> p a h", a=4)
        bm = carve(16)
        bmTb = carve(16 * 64 // 2, BF16).rearrange("p (b c) -> p b c", b=16)
        bmb = carve(8, BF16)
        mod = carve(DEPTH * 96 * 17).rearrange("p (l c b) -> p l c b", l=DEPTH, c=96)
        A = carve(DEPTH * 2 * 16 * 17).rearrange("p (l s k b) -> p l s k b", l=DEPTH, s=2, k=16)
        lnw = carve(DEPTH * 2 * 16).rearrange("p (l s k) -> p l s k", l=DEPTH, s=2)
        scT = carve(16 * 18 // 2, BF16).rearrange("p (k b) -> p k b", k=16)
        qkw = carve(DEPTH * 2 * 128).rearrange("p (l s d) -> p l s d", l=DEPTH, s=2)
        esink = carve(DEPTH * 16).rearrange("p (l h) -> p l h", l=DEPTH)
        kprev = carve(DEPTH * 4 * 128 // 2, BF16).rearrange("p (l u k) -> p l u k", l=DEPTH, u=4)
        vprev = carve(DEPTH * 4 * 130 // 2, BF16).rearrange("p (l u k) -> p l u k", l=DEPTH, u=4)
        r_const = P.res("const")
        r_mod = P.res("mod")
        r_carry = P.res("carry")
        persist_end = off[0]

        xT = carve(16 * TP).rearrange("p (k t) -> p k t", k=16)
        hT = carve(16 * TP // 2, BF16).rearrange("p (k t) -> p k t", k=16)
        mg = carve(4 * 2048 // 2, BF16).rearrange("p (t c) -> p t c", t=4)
        wts = [carve(16 * 512 // 2, BF16).rearrange("p (k c) -> p k c", k=16) for _ in range(2)]
        r_mg = P.res("mg")
        r_xk = [P.res("x%d" % i) for i in range(16)]
        r_hk = [P.res("h%d" % i) for i in range(16)]
        r_w = [P.res("w0"), P.res("w1")]
        sgm = carve(4 * 512).rearrange("p (t c) -> p t c", t=4)
        r_sgm = P.res("sgm")
        xin = sgm.rearrange("p t c -> p (t c)")
        r_xin = r_sgm
        r_rotA, r_rotB = P.res("rotA"), P.res("rotB")
        r_bf = (P.res("bf0"), P.res("bf1"))
        tmp = [carve(512) for _ in range(6)]
        r_tmp = [P.res("t%d" % i) for i in range(6)]
        sm = [carve(8) for _ in range(6)]
        r_sm = [P.res("sm%d" % i) for i in range(6)]
        sq = carve(2 * TP // 2, BF16).rearrange("p (k t) -> p k t", k=2)
        r_sq = P.res("sq")
        rsd = carve(TP)
        r_rsd = P.res("rsd")
        cst = carve(2 * 4 * 128).rearrange("p (a t d) -> p a t d", a=2, t=4)
        r_cst = P.res("cst")
        ub = off[0]
        qdT = carve(4 * TP // 2, BF16).rearrange("p (c t) -> p c t", c=4)
        kdT = carve(4 * TP // 2, BF16).rearrange("p (c t) -> p c t", c=4)
        kd = carve(4 * 512 // 2, BF16).rearrange("p (t c) -> p t c", t=4)
        vv = carve(4 * 512 // 2, BF16).rearrange("p (t c) -> p t c", t=4)
        gm = carve(4 * 512).rearrange("p (t c) -> p t c", t=4)
        r_qdT, r_kdT, r_kd, r_vv, r_gm = P.res("qdT"), P.res("kdT"), P.res("kd"), P.res("vv"), P.res("gm")
        kT = carve((128 + TP) // 2, BF16)
        vext = carve(5 * 130 // 2, BF16).rearrange("p (t c) -> p t c", t=5)
        r_kT, r_vext = P.res("kT"), P.res("vext")
        S = carve(2 * 256).rearrange("p (c v) -> p c v", c=2)
        Sb = carve(2 * 256 // 2, BF16).rearrange("p (c v) -> p c v", c=2)
        r_S, r_Sb = P.res("S"), P.res("Sb")
        pbuf = [carve(512 // 2, BF16) for _ in range(4)]
        r_pbuf = [P.res("pb%d" % i) for i in range(4)]
        attb = [carve(64, BF16) for _ in range(2)]
        r_attb = [P.res("attb0"), P.res("attb1")]
        qaT = carve(4 * TP // 2, BF16).rearrange("p (c t) -> p c t", c=4)
        r_qaT = P.res("qaT")
        ug = [qaT] * 2
        r_ug = [r_qaT] * 2
        qexp = [carve(2 * 64 // 2, BF16).rearrange("p (c t) -> p c t", c=2) for _ in range(2)]
        kexp = [carve(256 // 2, BF16) for _ in range(2)]
        pexp = [carve(64 // 2, BF16) for _ in range(2)]
        r_qexp = [P.res("qexp0"), P.res("qexp1")]
        r_kexp = [P.res("kexp0"), P.res("kexp1")]
        r_pexp = [P.res("pexp0"), P.res("pexp1")]
        s0 = [carve(512).rearrange("p (c v) -> p c v", c=2) for _ in range(2)]
        s0b = [carve(256, BF16).rearrange("p (c v) -> p c v", c=2) for _ in range(2)]
        r_s0 = [P.res("s0a"), P.res("s0b")]
        r_s0b = [P.res("s0ba"), P.res("s0bb")]
        cst_k = [carve(128) for _ in range(2)]
        cst_v = [carve(128) for _ in range(2)]
        r_cstk = [P.res("cka"), P.res("ckb")]
        r_cstv = [P.res("cva"), P.res("cvb")]
        kcb = [carve(64, BF16) for _ in range(2)]
        kcT = [carve(64, BF16) for _ in range(2)]
        r_kcb = [P.res("kcb0"), P.res("kcb1")]
        r_kcT = [P.res("kcT0"), P.res("kcT1")]
        vcall = mg[:, 1:4, :].rearrange("p t c -> p (t c)")[:, 0:16 * 130].rearrange("p (b c) -> p b c", b=16)
        r_vcall = P.res("vcall")
        kvst = carve(256)
        r_kvst = P.res("kvst")
        gfree = gm[:, 1:4, :].rearrange("p t c -> p (t c)")
        kfree = kd[:, 1:4, :].rearrange("p t c -> p (t c)")
        vfree = vv[:, 1:4, :].rearrange("p t c -> p (t c)")
        s0 = s0 + [gfree[:, 0:512].rearrange("p (c v) -> p c v", c=2), gfree[:, 512:1024].rearrange("p (c v) -> p c v", c=2)]
        r_s0 = r_s0 + [P.res("s0c"), P.res("s0d")]
        s0b = s0b + [kfree[:, 0:512].rearrange("p (c v) -> p c v", c=2), kfree[:, 512:1024].rearrange("p (c v) -> p c v", c=2)]
        r_s0b = r_s0b + [P.res("s0bc"), P.res("s0bd")]
        cst_k = cst_k + [gfree[:, 1024:1152], gfree[:, 1152:1280]]
        cst_v = cst_v + [gfree[:, 1280:1408], gfree[:, 1408:1536]]
        r_cstk = r_cstk + [P.res("ckc"), P.res("ckd")]
        r_cstv = r_cstv + [P.res("cvc"), P.res("cvd")]
        sfree = sgm[:, 1:4, :].rearrange("p t c -> p (t c)")
        s0 = s0 + [sfree[:, 0:512].rearrange("p (c v) -> p c v", c=2), sfree[:, 512:1024].rearrange("p (c v) -> p c v", c=2)]
        r_s0 = r_s0 + [P.res("s0e"), P.res("s0f")]
        s0b = s0b + [sfree[:, 1024:1280].bitcast(BF16).rearrange("p (c v) -> p c v", c=2),
                     sfree[:, 1280:1536].bitcast(BF16).rearrange("p (c v) -> p c v", c=2)]
        r_s0b = r_s0b + [P.res("s0be"), P.res("s0bf")]
        kcb = kcb + [vfree[:, 0:128], vfree[:, 128:256]]
        kcT = kcT + [vfree[:, 256:384], vfree[:, 384:512]]
        r_kcb = r_kcb + [P.res("kcb2"), P.res("kcb3")]
        r_kcT = r_kcT + [P.res("kcT2"), P.res("kcT3")]
        print("arena words used", off[0])

        r_dram = {}

        def rd(name):
            if name not in r_dram:
                r_dram[name] = P.res("dr_" + name)
            return r_dram[name]

        dma("sp", ident, masks_d[4], [], [r_const], r_const)
        cp(identb, ident, [r_const], [r_const])
        dma("sp", mstage, masks_d[5], [], [r_const], r_const)
        cp(onesb, mstage, [r_const], [r_const])
        for i in range(4):
            dma("sp", mstage, masks_d[i], [], [r_const], r_const)
            cp(mk[i], mstage, [r_const], [r_const])
        P.op("dve", lambda E: E.memset(epsc, EPS), [], [r_const])
        for a in range(4):
            dma("sp", dec[:, a, :], dec_d[a], [], [r_const], r_const)
            dma("sp", decs[:, a, :], decs_d[a], [], [r_const], r_const)
        dma("sp", bm, bm_d, [], [r_const], r_const)
        cp(bmb, bm, [r_const], [r_const])
        for b4 in range(4):
            dma("sp", xin[:, 0:256].rearrange("p (b c) -> p b c", b=4), bmT_d[:, b4 * 4:(b4 + 1) * 4, :], [], [r_xin], r_xin)
            cp(bmTb[:, b4 * 4:(b4 + 1) * 4, :], xin[:, 0:256].rearrange("p (b c) -> p b c", b=4), [r_xin], [r_const])
        for l in range(DEPTH):
            dma("sp", qkw[:, l, 0, :], qnw[l:l + 1, :].to_broadcast([128, 128]), [], [r_const], r_const)
            dma("sp", qkw[:, l, 1, :], knw[l:l + 1, :].to_broadcast([128, 128]), [], [r_const], r_const)
            dma("sp", esink[:, l, :], sinks[l:l + 1, :].to_broadcast([128, 16]), [], [r_const], r_const)
        act(esink, esink, AF.Exp, [r_const], [r_const])
        for l in range(DEPTH):
            for s_, src in enumerate((n1w, n2w)):
                dma("sp", xin[0:16, 0:128], src[l], [], [r_xin], r_xin)
                pb, pr = nb()
                tr(pb[:, 0:16], xin[0:16, 0:128], ident[0:16, 0:16], [r_xin, r_const], [pr])
                cp(lnw[:, l, s_, :], pb[:, 0:16], [pr], [r_mod])
        dma("sp", xin[0:17, :], cv, [], [r_xin], r_xin)
        act(tmp[0][0:17, :], xin[0:17, 0:512], AF.Sigmoid, [r_xin], [r_tmp[0]])
        cfull = xT[:, 0:4, :].rearrange("p k t -> p (k t)")
        r_cf = None
        act(cfull[0:17, :], xin[0:17, :], AF.Sigmoid, [r_xin], r_xk[0:4])
        tt(cfull[0:17, :], cfull[0:17, :], xin[0:17, :], ALU.mult, r_xk[0:4] + [r_xin], r_xk[0:4])
        for k in range(16):
            pb, pr = nb()
            tr(pb[:, 0:17], cfull[0:17, k * 128:(k + 1) * 128], ident[0:17, 0:17], r_xk[0:4] + [r_const], [pr])
            cp(scT[:, k, 0:17], pb[:, 0:17], [pr], [r_mod])
        wctr = [0]

        wmode = ["none"]
        wtid = [0]
        wb_pending = []

        def flush_wb():
            while wb_pending:
                s_, tid_, n_, down_ = wb_pending.pop(0)
                if down_:
                    dma("pool", wbf[tid_], wts[s_][:].rearrange("p k c -> p (k c)"), [r_w[s_]], [], r_w[s_])
                else:
                    dma("pool", wbf[tid_].rearrange("p (k c) -> p k c", k=16)[:, :, 0:n_], wts[s_][:, :, 0:n_], [r_w[s_]], [], r_w[s_])

        def wload(src_ap, ncols, extra=None, down=False):
            s = wctr[0] % 2
            wctr[0] += 1
            n = ncols if extra is None else 2 * ncols
            tid = wtid[0]
            if wmode[0] != "none":
                wtid[0] += 1
            mode = wmode[0]
            if mode == "p0":
                mode = "write" if tid % 2 == 0 else "cast"
            elif mode == "p1":
                mode = "read" if tid % 2 == 0 else "write"
            if mode == "read":
                if down:
                    dma("pool", wts[s][:].rearrange("p k c -> p (k c)"), wbf[tid], [], [r_w[s]], r_w[s])
                else:
                    dma("pool", wts[s][:, :, 0:n], wbf[tid].rearrange("p (k c) -> p k c", k=16)[:, :, 0:n], [], [r_w[s]], r_w[s])
                flush_wb()
                return s
            if down:
                wdv_ = wts[s][:].rearrange("p k c -> p (k c)").rearrange("p (f c) -> p f c", f=4)
                dma("pool", wdv_, src_ap.rearrange("(f p) c -> p f c", p=128), [], [r_w[s]], r_w[s])
            else:
                dma("pool", wts[s][:, :, 0:ncols], src_ap.rearrange("(k p) c -> p k c", p=128), [], [r_w[s]], r_w[s])
                if extra is not None:
                    dma("pool", wts[s][:, :, ncols:2 * ncols], extra.rearrange("(k p) c -> p k c", p=128), [], [r_w[s]], r_w[s])
            flush_wb()
            if mode == "write":
                wb_pending.append((s, tid, n, down))
            return s

        for l in range(DEPTH):
            dma("sp", xin[0:96, 0:128], b_ada[l], [], [r_xin], r_xin)
            pb, pr = nb()
            tr(pb[:, 0:96], xin[0:96, 0:128], ident[0:96, 0:96], [r_xin, r_const], [pr])
            cp(tmp[1][:, 0:96], pb[:, 0:96], [pr], [r_tmp[1]])
            for ct in range(24):
                s = wload(w_ada[l][:, ct * 512:(ct + 1) * 512], 512)
                pb, pr = nb()
                for c4 in range(4):
                    for k in range(16):
                        mm(pb[:, c4 * 17:(c4 + 1) * 17], wts[s][:, k, c4 * 128:(c4 + 1) * 128], scT[:, k, 0:17],
                           k == 0, k == 15, [r_w[s], r_mod], [pr])
                tt(mod[:, l, ct * 4:(ct + 1) * 4, :], pb[:, 0:68].rearrange("p (c b) -> p c b", c=4),
                   tmp[1][:, ct * 4:(ct + 1) * 4].unsqueeze(2).to_broadcast([128, 4, 17]), ALU.add,
                   [pr, r_tmp[1]], [r_mod])
            for s_ in range(2):
                kind = 1 if s_ == 0 else 4
                ts(A[:, l, s_], mod[:, l, kind * 16:(kind + 1) * 16, :], 1.0, None, ALU.add, None, [r_mod], [r_mod])
                tt(A[:, l, s_], A[:, l, s_], lnw[:, l, s_, :].unsqueeze(2).to_broadcast([128, 16, 17]), ALU.mult,
                   [r_mod], [r_mod])

        def run_pass(kind, tok0):
            samp = kind == "s"
            T = NB * LS if samp else TP
            M = 64 if samp else 128
            NT = 1 if samp else TP // 128
            xsrc = xs if samp else xp
            ydst = ys if samp else yp
            DEC = decs if samp else dec
            first = (not samp) and tok0 == 0
            last = (not samp) and tok0 + TP == SEQ

            for t in range(NT):
                dma("sp", xin[0:M, :], xsrc[tok0 + t * 128: tok0 + t * 128 + M, :], [], [r_xin], r_xin)
                for kb in range(4):
                    pb, pr = nb()
                    for k4 in range(4):
                        k = kb * 4 + k4
                        tr(pb[:, k4 * 128:k4 * 128 + M], xin[0:M, k * 128:(k + 1) * 128], ident[0:M, 0:M],
                           [r_xin, r_const], [pr])
                    act(xT[:, kb * 4:(kb + 1) * 4, t * 128:t * 128 + M],
                        pb[:].rearrange("p (k c) -> p k c", k=4)[:, :, 0:M], AF.Copy, [pr], r_xk[kb * 4:(kb + 1) * 4])
            for a in range(2):
                for t in range(NT):
                    src = cs_s[a] if samp else cs_p[a][tok0 + t * 128: tok0 + (t + 1) * 128, :]
                    dma("sp", cst[0:M, a, t, :], src, [], [r_cst], r_cst)

            def modulate(l, s_):
                shk = 0 if s_ == 0 else 3
                pb, pr = nb()
                for kb in range(8):
                    act(sq[:, :, 0:T], xT[:, kb * 2:(kb + 1) * 2, 0:T], AF.Square, r_xk[kb * 2:(kb + 1) * 2], [r_sq])
                    for k4 in range(2):
                        mm(pb[:, 0:T], onesb, sq[:, k4, 0:T], kb == 0 and k4 == 0, kb == 7 and k4 == 1,
                           [r_sq, r_const], [pr])
                rstd_from(rsd[:, 0:T], pb[:, 0:T], float(D), [pr, r_const], [r_rsd], tmp[0][:, 0:T], r_tmp[0])
                for k in range(16):
                    i = k % 2
                    tt(tmp[1 + i][:, 0:T], xT[:, k, 0:T], rsd[:, 0:T], ALU.mult, [r_xk[k], r_rsd], [r_tmp[1 + i]])
                    if not samp:
                        act(hT[:, k, 0:T], tmp[1 + i][:, 0:T], AF.Identity, [r_tmp[1 + i], r_mod], [r_hk[k]],
                            scale=A[:, l, s_, k, 0:1], bias=mod[:, l, shk * 16 + k, 0:1])
                    else:
                        v3 = tmp[1 + i][:, 0:T].rearrange("p (b t) -> p b t", t=LS)
                        tt(v3, v3, A[:, l, s_, k, 1:17].unsqueeze(2).to_broadcast([128, NB, LS]), ALU.mult,
                           [r_tmp[1 + i], r_mod], [r_tmp[1 + i]])
                        tt(hT[:, k, 0:T].rearrange("p (b t) -> p b t", t=LS), v3,
                           mod[:, l, shk * 16 + k, 1:17].unsqueeze(2).to_broadcast([128, NB, LS]), ALU.add,
                           [r_tmp[1 + i], r_mod], [r_hk[k]])

            def resid_add(l, gk, k, pb, pr):
                if not samp:
                    stt(xT[:, k, 0:T], pb[:, 0:T], mod[:, l, gk * 16 + k, 0:1], xT[:, k, 0:T], ALU.mult, ALU.add,
                        [pr, r_mod, r_xk[k]], [r_xk[k]])
                else:
                    v3 = tmp[4][:, 0:T].rearrange("p (b t) -> p b t", t=LS)
                    tt(v3, pb[:, 0:T].rearrange("p (b t) -> p b t", t=LS),
                       mod[:, l, gk * 16 + k, 1:17].unsqueeze(2).to_broadcast([128, NB, LS]), ALU.mult,
                       [pr, r_mod], [r_tmp[4]])
                    tt(xT[:, k, 0:T], xT[:, k, 0:T], tmp[4][:, 0:T], ALU.add, [r_xk[k], r_tmp[4]], [r_xk[k]])

            pending = []

            def flush_pending():
                while pending:
                    pending.pop(0)()

            def proj(l, col0, ncols, evac, extra_col0=None):
                ex = None if extra_col0 is None else w_in[l][:, extra_col0:extra_col0 + ncols]
                s = wload(w_in[l][:, col0:col0 + ncols], ncols, ex)
                n = ncols if ex is None else 2 * ncols
                for t in range(NT):
                    pb, pr = nb()
                    for k in range(16):
                        mm(pb[0:M, 0:n], hT[:, k, t * 128:t * 128 + M], wts[s][:, k, 0:n], k == 0, k == 15,
                           [r_hk[k], r_w[s]], [pr])
                    flush_pending()
                    evac(t, pb, pr)
                    yield

            def drive(fgs, bg=None, k=1):
                for fg in fgs:
                    for _ in fg:
                        for _i in range(k):
                            if bg is not None:
                                try:
                                    next(bg)
                                except StopIteration:
                                    bg = None
                flush_pending()
                if bg is not None:
                    for _ in bg:
                        pass

            rotA, rotB = tmp[0][:, 0:256], tmp[0][:, 256:512]
            rot_out = (tmp[2], tmp[1])
            r_rot_out = (r_tmp[2], r_tmp[1])
            bfbuf = (tmp[3][:, 0:256].bitcast(BF16), tmp[3][:, 256:512].bitcast(BF16))

            def rotary(t, pb, pr, out_f32, r_out):
                pv = pb[0:M, :].rearrange("p (h s d) -> p h s d", h=2, s=2)
                ov = out_f32[0:M, :].rearrange("p (h s d) -> p h s d", h=2, s=2)
                cosb = cst[0:M, 0, t, :].unsqueeze(1).to_broadcast([M, 2, 128])
                sinb = cst[0:M, 1, t, :].unsqueeze(1).to_broadcast([M, 2, 128])
                a_ = rotA[0:M, :].rearrange("p (h d) -> p h d", h=2)
                b_ = rotB[0:M, :].rearrange("p (h d) -> p h d", h=2)
                tt(a_, pv[:, :, 0, :], cosb, ALU.mult, [pr, r_cst], [r_rotA])
                tt(b_, pv[:, :, 1, :], sinb, ALU.mult, [pr, r_cst], [r_rotB])
                tt(ov[:, :, 0, :], a_, b_, ALU.subtract, [r_rotA, r_rotB], [r_out])
                tt(a_, pv[:, :, 0, :], sinb, ALU.mult, [pr, r_cst], [r_rotA])
                tt(b_, pv[:, :, 1, :], cosb, ALU.mult, [pr, r_cst], [r_rotB])
                tt(ov[:, :, 1, :], a_, b_, ALU.add, [r_rotA, r_rotB], [r_out])

            def transposes4(src_bf, r_src, dstT, r_dst, t, ncol=4):
                def go():
                    pb, pr = nb()
                    pbb = pb[:].bitcast(BF16)
                    for c in range(ncol):
                        tr(pbb[:, c * 128:c * 128 + M], src_bf[0:M, c * 128:(c + 1) * 128], identb[0:M, 0:M],
                           [r_src, r_const], [pr])
                    if ncol == 4:
                        act(dstT[:, :, t * 128:t * 128 + M], pbb[:, 0:512].rearrange("p (c m) -> p c m", c=4)[:, :, 0:M],
                            AF.Copy, [pr], [r_dst])
                    else:
                        act(dstT[:, 128 + t * 128:128 + t * 128 + M], pbb[:, 0:M], AF.Copy, [pr], [r_dst])
                pending.append(go)

            scale = 128.0 ** -0.5

            def layer(l):
                modulate(l, 0)

                def mk_ret_evacs(u):
                    def ev_q(t, pb, pr):
                        p_ = t % 2
                        rotary(t, pb, pr, rot_out[p_], r_rot_out[p_])
                        tt(bfbuf[p_][0:M, :].rearrange("p (h d) -> p h d", h=2),
                           rot_out[p_][0:M, :].rearrange("p (h d) -> p h d", h=2),
                           DEC[0:M, 0, 2 * u:2 * u + 2].unsqueeze(2).to_broadcast([M, 2, 256]), ALU.mult,
                           [r_rot_out[p_], r_const], [r_bf[p_]])
                        transposes4(bfbuf[p_], r_bf[p_], qdT, r_qdT, t)

                    def ev_k(t, pb, pr):
                        p_ = t % 2
                        rotary(t, pb, pr, rot_out[p_], r_rot_out[p_])
                        tt(bfbuf[p_][0:M, :].rearrange("p (h d) -> p h d", h=2),
                           rot_out[p_][0:M, :].rearrange("p (h d) -> p h d", h=2),
                           DEC[0:M, 1, 2 * u:2 * u + 2].unsqueeze(2).to_broadcast([M, 2, 256]), ALU.mult,
                           [r_rot_out[p_], r_const], [r_bf[p_]])
                        tt(kd[0:M, t, :].rearrange("p (h d) -> p h d", h=2),
                           rot_out[p_][0:M, :].rearrange("p (h d) -> p h d", h=2),
                           DEC[0:M, 2, 2 * u:2 * u + 2].unsqueeze(2).to_broadcast([M, 2, 256]), ALU.mult,
                           [r_rot_out[p_], r_const], [r_kd])
                        transposes4(bfbuf[p_], r_bf[p_], kdT, r_kdT, t)

                    def ev_v(t, pb, pr):
                        act(vv[0:M, t, :], pb[0:M, :], AF.Copy, [pr], [r_vv])

                    def ev_g(t, pb, pr):
                        act(tmp[4][0:M, :], pb[0:M, :], AF.Sigmoid, [pr], [r_tmp[4]])
                        tt(gm[0:M, t, :], tmp[4][0:M, :], pb[0:M, :], ALU.mult, [pr, r_tmp[4]], [r_gm])

                    def ev_m(t, pb, pr):
                        act(tmp[4][0:M, :], pb[0:M, :], AF.Sigmoid, [pr], [r_tmp[4]])
                        tt(gm[0:M, t, :], gm[0:M, t, :], tmp[4][0:M, :], ALU.mult, [r_gm, r_tmp[4]], [r_gm])
                    return ev_q, ev_k, ev_v, ev_g, ev_m

                def ret_norm_gate(po, pro, t, u, hh):
                    act(tmp[5][0:M, 0:256], po[0:M, 0:256], AF.Square, [pro], [r_tmp[5], r_sm[0]], accum_out=sm[0][0:M, 0:1])
                    rstd_from(sm[2][0:M, 0:1], sm[0][0:M, 0:1], 256.0, [r_sm[0], r_const], [r_sm[2]], sm[1][0:M, 0:1], r_sm[1])
                    stt(mg[0:M, t, u * 512 + hh * 256: u * 512 + (hh + 1) * 256], po[0:M, 0:256], sm[2][0:M, 0:1],
                        gm[0:M, t, hh * 256:(hh + 1) * 256], ALU.mult, ALU.mult, [pro, r_sm[2], r_gm], [r_mg])

                def ret_sweep_prompt(u):
                    for hh in range(2):
                        H = 2 * u + hh
                        if first:
                            P.op("dve", lambda E: E.memset(S[:].rearrange("p c v -> p (c v)"), 0.0), [], [r_S])
                        else:
                            dma("sp", S, st_p[l, H].rearrange("(c p) v -> p c v", p=128), [rd("st_p")], [r_S], r_S)
                        cp(Sb, S, [r_S], [r_Sb], eng="act")
                        att = {}

                        def emit_att(t):
                            tsl = slice(t * 128, (t + 1) * 128)
                            pa, pra = nb()
                            for c in range(2):
                                mm(pa[:, 0:128], kdT[:, 2 * hh + c, tsl], qdT[:, 2 * hh + c, tsl], c == 0, c == 1,
                                   [r_kdT, r_qdT], [pra])
                            i = t % 2
                            tt(attb[i], pa[:, 0:128], mk[0], ALU.mult, [pra, r_const], [r_attb[i]])
                        emit_att(0)
                        yield
                        for t in range(NT):
                            tsl = slice(t * 128, (t + 1) * 128)
                            i = t % 2
                            po, pro = nb()
                            mm(po[:, 0:256], attb[i], vv[:, t, hh * 256:(hh + 1) * 256], True, False, [r_attb[i], r_vv], [pro])
                            for c in range(2):
                                mm(po[:, 0:256], qdT[:, 2 * hh + c, tsl], Sb[:, c, :], False, c == 1, [r_qdT, r_Sb], [pro])
                            ret_norm_gate(po, pro, t, u, hh)
                            for c in range(2):
                                pu, pru = nb()
                                mm(pu[:, 0:256], kd[:, t, hh * 256 + c * 128: hh * 256 + (c + 1) * 128],
                                   vv[:, t, hh * 256:(hh + 1) * 256], True, True, [r_kd, r_vv], [pru])
                                stt(S[:, c, :], S[:, c, :], DEC[:, 3, H:H + 1], pu[:, 0:256], ALU.mult, ALU.add,
                                    [r_S, pru, r_const], [r_S])
                            if t + 1 < NT:
                                cp(Sb, S, [r_S], [r_Sb], eng="act")
                                emit_att(t + 1)
                            yield
                        dma("sp", st_p[l, H].rearrange("(c p) v -> p c v", p=128), S, [r_S], [rd("st_p")], r_S)

                def ret_sweep_sample(u):
                    for hh in range(2):
                        H = 2 * u + hh
                        pa, pra = nb()
                        for c in range(2):
                            mm(pa[0:M, 0:M], kdT[:, 2 * hh + c, 0:M], qdT[:, 2 * hh + c, 0:M], c == 0, c == 1,
                               [r_kdT, r_qdT], [pra])
                        tt(attb[0][0:M, 0:M], pa[0:M, 0:M], mk[2][0:M, 0:M], ALU.mult, [pra, r_const], [r_attb[0]])
                        po, pro = psb[7], psr[7]
                        mm(po[0:M, 0:256], attb[0][0:M, 0:M], vv[0:M, 0, hh * 256:(hh + 1) * 256], True, False,
                           [r_attb[0], r_vv], [pro])
                        KS = 6

                        def ld(b):
                            j = b % KS
                            dma("sp", s0[j], st_in[l, b, H].rearrange("(c p) v -> p c v", p=128), [r_qdT], [r_s0[j]], r_s0[j])
                        PD = 3
                        for b in range(PD):
                            ld(b)
                        for b in range(NB):
                            if b + PD < NB:
                                ld(b + PD)
                            j = b % KS
                            i = b % 2
                            cp(s0b[j], s0[j], [r_s0[j]], [r_s0b[j]], eng="act")
                            tt(qexp[i], qdT[:, 2 * hh:2 * hh + 2, 0:64], bmTb[:, b, :].unsqueeze(1).to_broadcast([128, 2, 64]),
                               ALU.mult, [r_qdT, r_const], [r_qexp[i]])
                            ts(kexp[i][0:64, :], kd[0:64, 0, hh * 256:(hh + 1) * 256], bm[0:64, b:b + 1], None, ALU.mult, None,
                               [r_kd, r_const], [r_kexp[i]])
                            for c in range(2):
                                mm(po[0:M, 0:256], qexp[i][:, c, :], s0b[j][:, c, :], False,
                                   b == NB - 1 and c == 1, [r_qexp[i], r_s0b[j]], [pro])
                            for c in range(2):
                                pu, pru = nb()
                                mm(pu[:, 0:256], kexp[i][0:64, c * 128:(c + 1) * 128], vv[0:64, 0, hh * 256:(hh + 1) * 256],
                                   True, True, [r_kexp[i], r_vv], [pru])
                                stt(s0[j][:, c, :], s0[j][:, c, :], DEC[:, 3, H:H + 1], pu[:, 0:256], ALU.mult, ALU.add,
                                    [r_s0[j], pru, r_const], [r_s0[j]])
                            dma("sp", st_s[l, b, H].rearrange("(c p) v -> p c v", p=128), s0[j], [r_s0[j]], [rd("st_s")], r_s0[j])
                            if b % 2 == 1 and b < NB - 1:
                                yield
                        ret_norm_gate(po, pro, 0, u, hh)
                    yield

                def mk_swa_evacs(u):
                    def ev_qa(t, pb, pr):
                        p_ = t % 2
                        sqb, r_sqb = rot_out[p_], r_rot_out[p_]
                        act(sqb[0:M, :], pb[0:M, :], AF.Square, [pr], [r_sqb])
                        P.op("dve", lambda E: E.tensor_reduce(out=sm[3][0:M, 0:4], in_=sqb[0:M, :].rearrange("p (g d) -> p g d", g=4),
                                                              axis=AX.X, op=ALU.add), [r_sqb], [r_sm[3]])
                        rstd_from(sm[5][0:M, 0:4], sm[3][0:M, 0:4], 128.0, [r_sm[3], r_const], [r_sm[5]], sm[4][0:M, 0:4], r_sm[4])
                        tt(sqb[0:M, :].rearrange("p (g d) -> p g d", g=4), pb[0:M, :].rearrange("p (g d) -> p g d", g=4),
                           sm[5][0:M, 0:4].unsqueeze(2).to_broadcast([M, 4, 128]), ALU.mult, [pr, r_sm[5]], [r_sqb])
                        tt(bfbuf[p_][0:M, :].rearrange("p (g d) -> p g d", g=4),
                           sqb[0:M, :].rearrange("p (g d) -> p g d", g=4),
                           qkw[0:M, l, 0, :].unsqueeze(1).to_broadcast([M, 4, 128]), ALU.mult, [r_sqb, r_const], [r_bf[p_]])
                        transposes4(bfbuf[p_], r_bf[p_], qaT, r_qaT, t)

                    def ev_kv(t, pb, pr):
                        p_ = t % 2
                        sqb, r_sqb = rot_out[p_], r_rot_out[p_]
                        act(sqb[0:M, 0:128], pb[0:M, 0:128], AF.Square, [pr], [r_sqb, r_sm[3]], accum_out=sm[3][0:M, 0:1])
                        rstd_from(sm[5][0:M, 0:1], sm[3][0:M, 0:1], 128.0, [r_sm[3], r_const], [r_sm[5]], sm[4][0:M, 0:1], r_sm[4])
                        stt(kvst[0:M, 0:128], pb[0:M, 0:128], sm[5][0:M, 0:1], qkw[0:M, l, 1, :], ALU.mult, ALU.mult,
                            [pr, r_sm[5], r_const], [r_kvst])
                        act(kvst[0:M, 128:256], pb[0:M, 128:256], AF.Copy, [pr], [r_kvst])
                        cp(bfbuf[p_][0:M, 0:128], kvst[0:M, 0:128], [r_kvst], [r_bf[p_]])
                        cp(vext[0:M, t + 1, 0:128], kvst[0:M, 128:256], [r_kvst], [r_vext])
                        transposes4(bfbuf[p_], r_bf[p_], kT, r_kT, t, ncol=1)
                        if samp:
                            for b in range(NB):
                                dma("pool", cks[l, b, 124:128, u, :], kvst[b * 4:(b + 1) * 4, 0:128], [r_kvst], [rd("cks")], r_kvst)
                                dma("pool", cvs[l, b, 124:128, u, :], kvst[b * 4:(b + 1) * 4, 128:256], [r_kvst], [rd("cvs")], r_kvst)
                        elif last and t == NT - 1:
                            dma("sp", ckp[l, :, u, :], kvst[:, 0:128], [r_kvst], [rd("ckp")], r_kvst)
                            dma("sp", cvp[l, :, u, :], kvst[:, 128:256], [r_kvst], [rd("cvp")], r_kvst)

                    def ev_ma(t, pb, pr):
                        act(sgm[0:M, t, :], pb[0:M, :], AF.Sigmoid, [pr], [r_sgm])
                    return ev_qa, ev_kv, ev_ma

                def swa_finish(po, pro, t, u, g2, Mq):
                    pov = po[0:Mq, 0:260].rearrange("p (g d) -> p g d", g=2)
                    tt(sm[0][0:Mq, 0:2], pov[:, :, 128], esink[0:Mq, l, 4 * u + 2 * g2: 4 * u + 2 * g2 + 2], ALU.add,
                       [pro, r_const], [r_sm[0]])
                    recip(sm[1][0:Mq, 0:2], sm[0][0:Mq, 0:2], [r_sm[0]], [r_sm[1]])
                    for gg in range(2):
                        g = g2 * 2 + gg
                        stt(tmp[5][0:Mq, gg * 128:(gg + 1) * 128], po[0:Mq, gg * 130:gg * 130 + 128], sm[1][0:Mq, gg:gg + 1],
                            sgm[0:Mq, t, g * 128:(g + 1) * 128], ALU.mult, ALU.mult, [pro, r_sm[1], r_sgm], [r_tmp[5]])
                    c0 = u * 512 + g2 * 256
                    tt(mg[0:Mq, t, c0:c0 + 256], mg[0:Mq, t, c0:c0 + 256], tmp[5][0:Mq, 0:256], ALU.add, [r_mg, r_tmp[5]], [r_mg])

                def swa_sweep_prompt(u):
                    if not first:
                        cp(kT[:, 0:128], kprev[:, l, u, :], [r_carry], [r_kT])
                        cp(vext[:, 0, :], vprev[:, l, u, :], [r_carry], [r_vext])
                    cur = {}

                    def emit_scores(t):
                        tsl = slice(t * 128, (t + 1) * 128)
                        has_prev = not (first and t == 0)
                        blocks = [(1, mk[0])] + ([(0, mk[1])] if has_prev else [])
                        pbs = []
                        for bi, (off_t, msk) in enumerate(blocks):
                            j = (t % 2) * 2 + bi
                            pscore, prs = nb()
                            mm(pscore[:, :], kT[:, (t + off_t) * 128:(t + off_t + 1) * 128], qaT[:, :, tsl], True, True,
                               [r_kT, r_qaT], [prs])
                            act(pbuf[j], pscore[:, :], AF.Exp, [prs], [r_pbuf[j]], scale=scale)
                            tt(pbuf[j].rearrange("p (g q) -> p g q", g=4), pbuf[j].rearrange("p (g q) -> p g q", g=4),
                               msk.unsqueeze(1).to_broadcast([128, 4, 128]), ALU.mult, [r_pbuf[j], r_const], [r_pbuf[j]])
                            pbs.append((j, t + off_t))
                        cur[t] = pbs
                    emit_scores(0)
                    yield
                    for t in range(NT):
                        pbs = cur.pop(t)
                        for g2 in range(2):
                            po, pro = nb()
                            for gg in range(2):
                                g = g2 * 2 + gg
                                for n_, (j, vt) in enumerate(pbs):
                                    mm(po[:, gg * 130:gg * 130 + 129], pbuf[j][:, g * 128:(g + 1) * 128], vext[:, vt, 0:129],
                                       n_ == 0, n_ == len(pbs) - 1, [r_pbuf[j], r_vext], [pro])
                            swa_finish(po, pro, t, u, g2, 128)
                        if t + 1 < NT:
                            emit_scores(t + 1)
                        yield
                    cp(kprev[:, l, u, :], kT[:, TP:TP + 128], [r_kT], [r_carry])
                    cp(vprev[:, l, u, :], vext[:, NT, :], [r_vext], [r_carry])

                def swa_sweep_sample(u):
                    pss, prss = nb()
                    mm(pss[0:64, 0:256], kT[:, 128:192], qaT[:, :, 0:64], True, True, [r_kT, r_qaT], [prss])
                    act(pbuf[0][0:64, 0:256], pss[0:64, 0:256], AF.Exp, [prss], [r_pbuf[0]], scale=scale)
                    tt(pbuf[0][0:64, 0:256].rearrange("p (g q) -> p g q", g=4), pbuf[0][0:64, 0:256].rearrange("p (g q) -> p g q", g=4),
                       mk[2][0:64, 0:64].unsqueeze(1).to_broadcast([64, 4, 64]), ALU.mult, [r_pbuf[0], r_const], [r_pbuf[0]])
                    psp, prsp = psb[7], psr[7]
                    P.op("dve", lambda E: E.memset(vcall[:, :, 128:130], 1.0), [], [r_vcall])
                    KS = 4

                    def ldc(b):
                        i = b % KS
                        dma("sp", cst_k[i], ck_in[l, b, :, u, :], [], [r_cstk[i]], r_cstk[i])
                        dma("sp", cst_v[i], cv_in[l, b, :, u, :], [], [r_cstv[i]], r_cstv[i])
                    for b in range(2):
                        ldc(b)
                    for b in range(NB):
                        if b + 2 < NB:
                            ldc(b + 2)
                        i = b % KS
                        cp(kcb[i], cst_k[i], [r_cstk[i]], [r_kcb[i]])
                        cp(vcall[:, b, 0:128], cst_v[i], [r_cstv[i]], [r_vcall], eng="act")
                        pt_, prt = nb()
                        tr(pt_[:].bitcast(BF16)[:, 0:128], kcb[i], identb, [r_kcb[i], r_const], [prt])
                        act(kcT[i], pt_[:].bitcast(BF16)[:, 0:128], AF.Copy, [prt], [r_kcT[i]])
                        mm(psp[:, 0:256].rearrange("p (g q) -> p g q", g=4)[:, :, b * 4:(b + 1) * 4], kcT[i],
                           qaT[:, :, b * 4:(b + 1) * 4], True, True, [r_kcT[i], r_qaT], [prsp])
                        dma("pool", cks[l, b, 0:124, u, :], cst_k[i][4:128, :], [r_cstk[i]], [rd("cks")], r_cstk[i])
                        dma("pool", cvs[l, b, 0:124, u, :], cst_v[i][4:128, :], [r_cstv[i]], [rd("cvs")], r_cstv[i])
                        if b % 2 == 1:
                            yield
                    act(pbuf[1][:, 0:256], psp[:, 0:256], AF.Exp, [prsp], [r_pbuf[1]], scale=scale)
                    tt(pbuf[1][:, 0:256].rearrange("p (g q) -> p g q", g=4), pbuf[1][:, 0:256].rearrange("p (g q) -> p g q", g=4),
                       mk[3][:, 0:64].unsqueeze(1).to_broadcast([128, 4, 64]), ALU.mult, [r_pbuf[1], r_const], [r_pbuf[1]])
                    for g2 in range(2):
                        po, pro = nb()
                        for gg in range(2):
                            g = g2 * 2 + gg
                            for b in range(NB):
                                i = b % 2
                                tt(pexp[i], pbuf[1][:, g * 64:(g + 1) * 64], bmTb[:, b, :], ALU.mult,
                                   [r_pbuf[1], r_const], [r_pexp[i]])
                                mm(po[0:64, gg * 130:gg * 130 + 129], pexp[i], vcall[:, b, 0:129], b == 0, False,
                                   [r_pexp[i], r_vcall], [pro])
                            mm(po[0:64, gg * 130:gg * 130 + 129], pbuf[0][0:64, g * 64:(g + 1) * 64], vext[0:64, 1, 0:129],
                               False, True, [r_pbuf[0], r_vext], [pro])
                        swa_finish(po, pro, 0, u, g2, 64)
                    yield

                ret_sweep = ret_sweep_sample if samp else ret_sweep_prompt
                swa_sweep = swa_sweep_sample if samp else swa_sweep_prompt
                bg = None
                for u in range(4):
                    ev_q, ev_k, ev_v, ev_g, ev_m = mk_ret_evacs(u)
                    drive([proj(l, OQ + u * 512, 512, ev_q), proj(l, OK_ + u * 512, 512, ev_k),
                           proj(l, OV + u * 512, 512, ev_v), proj(l, OG + u * 512, 512, ev_g),
                           proj(l, OMR + u * 512, 512, ev_m)], bg, k=(3 if samp else 1))
                    bg = ret_sweep(u)
                    ev_qa, ev_kv, ev_ma = mk_swa_evacs(u)
                    P.op("dve", lambda E: E.memset(vext[:, :, 128:130], 1.0), [], [r_vext])
                    drive([proj(l, OQA + u * 512, 512, ev_qa),
                           proj(l, OKA + u * 128, 128, ev_kv, extra_col0=OVA + u * 128),
                           proj(l, OMA + u * 512, 512, ev_ma)], bg, k=(6 if samp else 1))
                    bg = swa_sweep(u)
                drive([], bg)

                for t in range(NT):
                    for kb in range(2):
                        pb, pr = nb()
                        pbb = pb[:].bitcast(BF16)
                        for k8 in range(8):
                            k = kb * 8 + k8
                            tr(pbb[:, k8 * 128:k8 * 128 + M], mg[0:M, t, k * 128:(k + 1) * 128], identb[0:M, 0:M],
                               [r_mg, r_const], [pr])
                        act(hT[:, kb * 8:(kb + 1) * 8, t * 128:t * 128 + M],
                            pbb[:, :].rearrange("p (k m) -> p k m", k=8)[:, :, 0:M], AF.Copy, [pr], r_hk[kb * 8:(kb + 1) * 8])
                for ct in range(4):
                    s = wload(w_out[l][:, ct * 512:(ct + 1) * 512], 512)
                    for c4 in range(4):
                        pb, pr = nb()
                        for k in range(16):
                            mm(pb[:, 0:T], wts[s][:, k, c4 * 128:(c4 + 1) * 128], hT[:, k, 0:T], k == 0, k == 15,
                               [r_w[s], r_hk[k]], [pr])
                        resid_add(l, 2, ct * 4 + c4, pb, pr)
                modulate(l, 1)
                for fg in range(16):
                    s = wload(w_up[l][:, fg * 512:(fg + 1) * 512], 512)
                    ui = fg % 2
                    for c4 in range(4):
                        pb, pr = nb()
                        for k in range(16):
                            mm(pb[:, 0:T], wts[s][:, k, c4 * 128:(c4 + 1) * 128], hT[:, k, 0:T], k == 0, k == 15,
                               [r_w[s], r_hk[k]], [pr])
                        act(tmp[4][:, 0:T], pb[:, 0:T], AF.Relu, [pr], [r_tmp[4]])
                        tt(ug[ui][:, c4, 0:T], tmp[4][:, 0:T], tmp[4][:, 0:T], ALU.mult, [r_tmp[4]], [r_ug[ui]])
                    s2 = wload(w_down[l][fg * 512:(fg + 1) * 512, :], 2048, down=True)
                    wdv = wts[s2][:].rearrange("p k c -> p (k c)").rearrange("p (f c) -> p f c", f=4)
                    for dc in range(16):
                        pb, pr = nb()
                        for f in range(4):
                            mm(pb[:, 0:T], wdv[:, f, dc * 128:(dc + 1) * 128], ug[ui][:, f, 0:T], f == 0, f == 3,
                               [r_w[s2], r_ug[ui]], [pr])
                        resid_add(l, 5, dc, pb, pr)

            for l in range(DEPTH):
                layer(l)

            for t in range(NT):
                for kb in range(4):
                    pb, pr = nb()
                    for k4 in range(4):
                        k = kb * 4 + k4
                        tr(pb[0:M, k4 * 128:(k4 + 1) * 128], xT[:, k, t * 128:t * 128 + M], ident, [r_xk[k], r_const], [pr])
                    act(xin[0:M, kb * 512:(kb + 1) * 512], pb[0:M, :], AF.Copy, [pr], [r_xin])
                dma("sp", ydst[tok0 + t * 128: tok0 + t * 128 + M, :], xin[0:M, :], [r_xin], [rd("y")], r_xin)

        for p_ in range(SEQ // TP):
            wmode[0] = "p0" if p_ == 0 else ("p1" if p_ == 1 else "read")
            wtid[0] = 0
            run_pass("p", p_ * TP)
            if p_ <= 1:
                flush_wb()
                P.barrier()
        P.barrier()
        wmode[0] = "read"
        wtid[0] = 0
        run_pass("s", 0)
        P.barrier()
        P.emit()
        print("stats", P.stats)
    return nc


_CACHE = {}


def kernel(x_prompt, x_sample, c_prompt, c_sample, state_ret, cache_k_win, cache_v_win,
           norm1_w, norm2_w, w_ada, b_ada, w_in, q_norm_w, k_norm_w, sinks, w_out, w_up, w_down):
    f = lambda a: np.ascontiguousarray(np.asarray(a, dtype=np.float32))
    if "nc" not in _CACHE:
        _CACHE["nc"] = build_nc()
        _CACHE["consts"] = _consts()
    nc = _CACHE["nc"]
    cst = _CACHE["consts"]
    x_prompt, x_sample, c_prompt, c_sample = f(x_prompt), f(x_sample), f(c_prompt), f(c_sample)
    state_ret, cache_k_win, cache_v_win = f(state_ret), f(cache_k_win), f(cache_v_win)
    shared = {
        "n1w": f(norm1_w).reshape(DEPTH, 16, 128), "n2w": f(norm2_w).reshape(DEPTH, 16, 128),
        "w_ada": f(w_ada), "b_ada": f(b_ada).reshape(DEPTH, 96, 128), "w_in": f(w_in),
        "qnw": f(q_norm_w), "knw": f(k_norm_w), "sinks": f(sinks),
        "w_out": f(w_out), "w_up": f(w_up), "w_down": f(w_down),
        "cs_p": cst["cs_p"], "cs_s": cst["cs_s"], "dec": cst["dec"], "decs": cst["decs"],
        "masks": cst["masks"], "bm": cst["bm"], "bmT": cst["bmT"],
    }
    in_maps = []
    for c in range(8):
        m = dict(shared)
        m["xp"] = x_prompt[c]
        m["xs"] = x_sample[NB * c:NB * (c + 1)].reshape(NB * LS, D)
        m["cv"] = np.concatenate([c_prompt[c:c + 1], c_sample[NB * c:NB * (c + 1)]], 0)
        m["st_in"] = np.ascontiguousarray(state_ret[:, NB * c:NB * (c + 1)])
        m["ck_in"] = np.ascontiguousarray(cache_k_win[:, NB * c:NB * (c + 1)])
        m["cv_in"] = np.ascontiguousarray(cache_v_win[:, NB * c:NB * (c + 1)])
        in_maps.append(m)
    res = run_bass_kernel_spmd(nc, in_maps, core_ids=list(range(8)))
    R = res.results
    y_p = np.stack([R[c]["yp"] for c in range(8)], 0)
    y_s = np.concatenate([R[c]["ys"].reshape(NB, LS, D) for c in range(8)], 0)
    st_p = np.stack([R[c]["st_p"] for c in range(8)], 1)
    ck_p = np.stack([R[c]["ckp"] for c in range(8)], 1)
    cv_p = np.stack([R[c]["cvp"] for c in range(8)], 1)
    st_s = np.concatenate([R[c]["st_s"] for c in range(8)], 1)
    ck_s = np.concatenate([R[c]["cks"] for c in range(8)], 1)
    cv_s = np.concatenate([R[c]["cvs"] for c in range(8)], 1)
    return (y_p, y_s, st_p, ck_p, cv_p, st_s, ck_s, cv_s)
```

```python
from contextlib import ExitStack
import numpy as np
import concourse.bass as bass
import concourse.mybir as mybir
from concourse.bass_utils import run_bass_kernel_spmd

F32 = mybir.dt.float32
BF16 = mybir.dt.bfloat16
ALU = mybir.AluOpType
AF = mybir.ActivationFunctionType
AX = mybir.AxisListType

D = 2048
SEQ = 2048
NB = 16
LS = 4
PAST = 8192
DEPTH = 2
EPS = 1e-6
TP = 512
OQ, OK_, OV, OG, OQA, OKA, OVA, OMR, OMA = 0, 2048, 4096, 6144, 8192, 10240, 10752, 11264, 13312
INW = 15360


class Res:
    __slots__ = ("name", "w", "r", "sem", "cnt")

    def __init__(self, name):
        self.name = name
        self.w = None
        self.r = {}
        self.sem = None
        self.cnt = 0


class Op:
    __slots__ = ("eng", "fn", "deps", "idx", "mark", "dma", "markval")

    def __init__(self, eng, fn):
        self.eng = eng
        self.fn = fn
        self.deps = []
        self.mark = False
        self.dma = None
        self.markval = 0


class Prog:
    ENGS = ("pe", "act", "dve", "pool", "sp")

    def __init__(self, nc, stack):
        self.nc = nc
        self.stack = stack
        self.ops = {e: [] for e in self.ENGS}
        self.known = {e: {} for e in self.ENGS}
        self.esem = {e: stack.enter_context(nc.semaphore("s_" + e)) for e in self.ENGS}
        self.nres = 0
        self.allres = []

    def res(self, name=None):
        self.nres += 1
        r = Res(name or "r%d" % self.nres)
        self.allres.append(r)
        return r

    def _apply(self, op, need):
        eng = op.eng
        kn = self.known[eng]
        for key, val in need.items():
            if isinstance(key, str):
                if key == eng:
                    if eng == "pe" or len(self.ops[eng]) - val > 2:
                        continue
                if kn.get(key, -1) >= val:
                    continue
                kn[key] = val
                self.ops[key][val].mark = True
                op.deps.append(("e", key, val))
            else:
                if kn.get(key, -1) >= val:
                    continue
                kn[key] = val
                op.deps.append(("d", key, val))

    @staticmethod
    def _acc(need, tok):
        if tok is None:
            return
        k, v = tok
        if need.get(k, -1) < v:
            need[k] = v

    def op(self, eng, fn, reads=(), writes=(), dma_key=None):
        o = Op(eng, fn)
        o.idx = len(self.ops[eng])
        need = {}
        for r in reads:
            self._acc(need, r.w)
        for r in writes:
            self._acc(need, r.w)
            for k, v in r.r.items():
                self._acc(need, (k, v))
        self._apply(o, need)
        if dma_key is not None:
            if dma_key.sem is None:
                dma_key.sem = self.stack.enter_context(self.nc.semaphore("d_" + dma_key.name))
            dma_key.cnt += 1
            o.dma = dma_key
            tok = (dma_key, dma_key.cnt * 16)
        else:
            tok = (eng, o.idx)
        for r in reads:
            if r.r.get(tok[0], -1) < tok[1]:
                r.r[tok[0]] = tok[1]
        for r in writes:
            r.w = tok
            r.r = {}
        self.ops[eng].append(o)
        return o

    def barrier(self):
        last = {}
        for e in self.ENGS:
            if self.ops[e]:
                for o in reversed(self.ops[e]):
                    if o.fn is not None and o.dma is None:
                        last[e] = o.idx
                        break
        for e in self.ENGS:
            o = Op(e, None)
            o.idx = len(self.ops[e])
            need = {k: v for k, v in last.items() if k != e}
            for r in self.allres:
                if r.sem is not None and r.cnt:
                    need[r] = r.cnt * 16
            self._apply(o, need)
            self.ops[e].append(o)

    def emit(self):
        nc = self.nc
        for e in self.ENGS:
            c = 0
            for o in self.ops[e]:
                if o.mark:
                    c += 1
                    o.markval = c
        self.stats = {e: (len(self.ops[e]), sum(1 for o in self.ops[e] if o.mark),
                          sum(len(o.deps) for o in self.ops[e])) for e in self.ENGS}

        def run(e, E):
            for o in self.ops[e]:
                for tok in o.deps:
                    if tok[0] == "e":
                        E.wait_ge(self.esem[tok[1]], self.ops[tok[1]][tok[2]].markval)
                    else:
                        E.wait_ge(tok[1].sem, tok[2])
                if o.fn is None:
                    continue
                ins = o.fn(E)
                if o.dma is not None:
                    ins.then_inc(o.dma.sem, 16)
                elif o.mark:
                    ins.then_inc(self.esem[e], 1)

        with nc.Block() as block:
            @block.tensor
            def _(E):
                run("pe", E)

            @block.scalar
            def _(E):
                run("act", E)

            @block.vector
            def _(E):
                run("dve", E)

            @block.gpsimd
            def _(E):
                run("pool", E)

            @block.sync
            def _(E):
                run("sp", E)


def _consts():
    half = 128
    inv = (1.0 / (10000.0 ** (np.arange(half, dtype=np.float32) / np.float32(half)))).astype(np.float32)
    pos_p = np.arange(SEQ, dtype=np.float32)
    ang = (pos_p[:, None] * inv[None, :]).astype(np.float32)
    cs_p = np.stack([np.cos(ang), np.sin(ang)], 0).astype(np.float32)
    pos_s = (PAST + (np.arange(NB * LS) % LS)).astype(np.float32)
    ang = (pos_s[:, None] * inv[None, :]).astype(np.float32)
    cs_s = np.stack([np.cos(ang), np.sin(ang)], 0).astype(np.float32)
    h = np.arange(8, dtype=np.float64)
    log_g = np.log1p(-np.exp2(-5.0 - h))
    i = np.arange(128, dtype=np.float64)
    dec = np.zeros((4, 128, 8), np.float32)
    dec[0] = np.exp((i[:, None] + 1.0) * log_g[None, :])
    dec[1] = np.exp(-(i[:, None] + 1.0) * log_g[None, :]) / 16.0
    dec[2] = np.exp((127.0 - i[:, None]) * log_g[None, :]) / 16.0
    dec[3] = np.exp(128.0 * log_g)[None, :]
    t = (np.arange(128) % LS).astype(np.float64)
    decs = np.zeros((4, 128, 8), np.float32)
    decs[0] = np.exp((t[:, None] + 1.0) * log_g[None, :])
    decs[1] = np.exp(-(t[:, None] + 1.0) * log_g[None, :]) / 16.0
    decs[2] = np.exp((LS - 1.0 - t[:, None]) * log_g[None, :]) / 16.0
    decs[3] = np.exp(float(LS) * log_g)[None, :]
    j = np.arange(128)
    m = np.zeros((6, 128, 128), np.float32)
    m[0] = (j[:, None] <= j[None, :])
    m[1] = (j[:, None] >= j[None, :])
    bb = j // LS
    tt = j % LS
    m[2] = (bb[:, None] == bb[None, :]) & (tt[:, None] <= tt[None, :])
    m[3] = (j[:, None] >= tt[None, :])
    m[4] = np.eye(128)
    m[5] = 1.0
    bm = np.zeros((128, 16), np.float32)
    bm[:64] = (bb[:64, None] == np.arange(16)[None, :])
    bmT = np.zeros((128, 16, 64), np.float32)
    bmT[:] = (np.arange(16)[:, None] == bb[None, :64])[None]
    return dict(cs_p=cs_p, cs_s=cs_s, dec=dec, decs=decs, masks=m, bm=bm, bmT=bmT)


def build_nc():
    nc = bass.Bass("TRN2", target_bir_lowering=False)

    def din(name, shape, dt=F32):
        return nc.dram_tensor(name, list(shape), dt, kind="ExternalInput").ap()

    def dout(name, shape):
        return nc.dram_tensor(name, list(shape), F32, kind="ExternalOutput").ap()

    xp = din("xp", [SEQ, D])
    xs = din("xs", [NB * LS, D])
    cv = din("cv", [NB + 1, D])
    st_in = din("st_in", [DEPTH, NB, 8, 256, 256])
    ck_in = din("ck_in", [DEPTH, NB, 128, 4, 128])
    cv_in = din("cv_in", [DEPTH, NB, 128, 4, 128])
    n1w = din("n1w", [DEPTH, 16, 128])
    n2w = din("n2w", [DEPTH, 16, 128])
    w_ada = din("w_ada", [DEPTH, D, 6 * D])
    b_ada = din("b_ada", [DEPTH, 96, 128])
    w_in = din("w_in", [DEPTH, D, INW])
    qnw = din("qnw", [DEPTH, 128])
    knw = din("knw", [DEPTH, 128])
    sinks = din("sinks", [DEPTH, 16])
    w_out = din("w_out", [DEPTH, D, D])
    w_up = din("w_up", [DEPTH, D, 4 * D])
    w_down = din("w_down", [DEPTH, 4 * D, D])
    cs_p = din("cs_p", [2, SEQ, 128])
    cs_s = din("cs_s", [2, NB * LS, 128])
    dec_d = din("dec", [4, 128, 8])
    decs_d = din("decs", [4, 128, 8])
    masks_d = din("masks", [6, 128, 128])
    bm_d = din("bm", [128, 16])
    bmT_d = din("bmT", [128, 16, 64])

    wbfL = [nc.dram_tensor("wbf%d" % i, [68, 128, 8192], BF16, kind="Internal").ap() for i in range(2)]

    class _W:
        def __getitem__(self, tid):
            return wbfL[tid // 68][tid % 68]
    wbf = _W()
    yp = dout("yp", [SEQ, D])
    ys = dout("ys", [NB * LS, D])
    st_p = dout("st_p", [DEPTH, 8, 256, 256])
    ckp = dout("ckp", [DEPTH, 128, 4, 128])
    cvp = dout("cvp", [DEPTH, 128, 4, 128])
    st_s = dout("st_s", [DEPTH, NB, 8, 256, 256])
    cks = dout("cks", [DEPTH, NB, 128, 4, 128])
    cvs = dout("cvs", [DEPTH, NB, 128, 4, 128])

    with ExitStack() as st:
        P = Prog(nc, st)
        AW = 52000
        arena = st.enter_context(nc.sbuf_tensor("arena", [128, AW], F32))
        psb = [st.enter_context(nc.psum_tensor("ps%d" % i, [128, 512], F32)) for i in range(8)]
        psr = [P.res("ps%d" % i) for i in range(8)]
        pctr = [0]

        def nb():
            i = pctr[0] % 7
            pctr[0] += 1
            return psb[i], psr[i]

        off = [0]

        def carve(nwords, dt=F32):
            a = off[0]
            off[0] += nwords
            assert off[0] <= AW, off[0]
            v = arena[:, a:a + nwords]
            return v.bitcast(BF16) if dt == BF16 else v

        def mm(out, lhsT, rhs, start, stop, reads, writes):
            P.op("pe", lambda E: E.matmul(out, lhsT=lhsT, rhs=rhs, start=start, stop=stop), reads, writes)

        def tr(out, in_, ident, reads, writes):
            P.op("pe", lambda E: E.transpose(out=out, in_=in_, identity=ident), reads, writes)

        def act(out, in_, func, reads, writes, **kw):
            P.op("act", lambda E: E.activation(out=out, in_=in_, func=func, **kw), reads, writes)

        def tt(out, in0, in1, op, reads, writes, eng="dve"):
            P.op(eng, lambda E: E.tensor_tensor(out=out, in0=in0, in1=in1, op=op), reads, writes)

        def ts(out, in0, s1, s2, op0, op1, reads, writes, eng="dve"):
            if op1 is None:
                P.op(eng, lambda E: E.tensor_scalar(out=out, in0=in0, scalar1=s1, scalar2=None, op0=op0), reads, writes)
            else:
                P.op(eng, lambda E: E.tensor_scalar(out=out, in0=in0, scalar1=s1, scalar2=s2, op0=op0, op1=op1), reads, writes)

        def stt(out, in0, scalar, in1, op0, op1, reads, writes):
            P.op("dve", lambda E: E.scalar_tensor_tensor(out=out, in0=in0, scalar=scalar, in1=in1, op0=op0, op1=op1), reads, writes)

        def cp(out, in_, reads, writes, eng="dve"):
            if eng == "act":
                P.op("act", lambda E: E.copy(out=out, in_=in_), reads, writes)
            else:
                P.op(eng, lambda E: E.tensor_copy(out=out, in_=in_), reads, writes)

        def recip(out, in_, reads, writes):
            P.op("dve", lambda E: E.reciprocal(out=out, in_=in_), reads, writes)

        def dma(q, out, in_, reads, writes, key):
            P.op(q, lambda E: E.dma_start(out=out, in_=in_), reads, writes, dma_key=key)

        def rstd_from(out, in_, n, reads, writes, tmp, r_tmp):
            act(tmp, in_, AF.Sqrt, reads, [r_tmp], bias=epsc[:in_.shape[0], :], scale=1.0 / n)
            recip(out, tmp, [r_tmp], writes)

        ident = carve(128)
        identb = carve(64, BF16)
        onesb = carve(64, BF16)
        mk = [carve(64, BF16) for _ in range(4)]
        mstage = carve(128)
        epsc = carve(1)
        dec = carve(32).rearrange("p (a h) -> p a h", a=4)
        decs = carve(32).rearrange("p (a h) -> p a h", a=4)
        bm = carve(16)
        bmTb = carve(16 * 64 // 2, BF16).rearrange("p (b c) -> p b c", b=16)
        bmb = carve(8, BF16)
        mod = carve(DEPTH * 96 * 17).rearrange("p (l c b) -> p l c b", l=DEPTH, c=96)
        A = carve(DEPTH * 2 * 16 * 17).rearrange("p (l s k b) -> p l s k b", l=DEPTH, s=2, k=16)
        lnw = carve(DEPTH * 2 * 16).rearrange("p (l s k) -> p l s k", l=DEPTH, s=2)
        scT = carve(16 * 18 // 2, BF16).rearrange("p (k b) -> p k b", k=16)
        qkw = carve(DEPTH * 2 * 128).rearrange("p (l s d) -> p l s d", l=DEPTH, s=2)
        esink = carve(DEPTH * 16).rearrange("p (l h) -> p l h", l=DEPTH)
        kprev = carve(DEPTH * 4 * 128 // 2, BF16).rearrange("p (l u k) -> p l u k", l=DEPTH, u=4)
        vprev = carve(DEPTH * 4 * 130 // 2, BF16).rearrange("p (l u k) -> p l u k", l=DEPTH, u=4)
        r_const = P.res("const")
        r_mod = P.res("mod")
        r_carry = P.res("carry")
        persist_end = off[0]

        xT = carve(16 * TP).rearrange("p (k t) -> p k t", k=16)
        hT = carve(16 * TP // 2, BF16).rearrange("p (k t) -> p k t", k=16)
        mg = carve(4 * 2048 // 2, BF16).rearrange("p (t c) -> p t c", t=4)
        wts = [carve(16 * 512 // 2, BF16).rearrange("p (k c) -> p k c", k=16) for _ in range(2)]
        r_mg = P.res("mg")
        r_xk = [P.res("x%d" % i) for i in range(16)]
        r_hk = [P.res("h%d" % i) for i in range(16)]
        r_w = [P.res("w0"), P.res("w1")]
        sgm = carve(4 * 512).rearrange("p (t c) -> p t c", t=4)
        r_sgm = P.res("sgm")
        xin = sgm.rearrange("p t c -> p (t c)")
        r_xin = r_sgm
        r_rotA, r_rotB = P.res("rotA"), P.res("rotB")
        r_bf = (P.res("bf0"), P.res("bf1"))
        tmp = [carve(512) for _ in range(6)]
        r_tmp = [P.res("t%d" % i) for i in range(6)]
        sm = [carve(8) for _ in range(6)]
        r_sm = [P.res("sm%d" % i) for i in range(6)]
        sq = carve(2 * TP // 2, BF16).rearrange("p (k t) -> p k t", k=2)
        r_sq = P.res("sq")
        rsd = carve(TP)
        r_rsd = P.res("rsd")
        cst = carve(2 * 4 * 128).rearrange("p (a t d) -> p a t d", a=2, t=4)
        r_cst = P.res("cst")
        ub = off[0]
        qdT = carve(4 * TP // 2, BF16).rearrange("p (c t) -> p c t", c=4)
        kdT = carve(4 * TP // 2, BF16).rearrange("p (c t) -> p c t", c=4)
        kd = carve(4 * 512 // 2, BF16).rearrange("p (t c) -> p t c", t=4)
        vv = carve(4 * 512 // 2, BF16).rearrange("p (t c) -> p t c", t=4)
        gm = carve(4 * 512).rearrange("p (t c) -> p t c", t=4)
        r_qdT, r_kdT, r_kd, r_vv, r_gm = P.res("qdT"), P.res("kdT"), P.res("kd"), P.res("vv"), P.res("gm")
        kT = carve((128 + TP) // 2, BF16)
        vext = carve(5 * 130 // 2, BF16).rearrange("p (t c) -> p t c", t=5)
        r_kT, r_vext = P.res("kT"), P.res("vext")
        S = carve(2 * 256).rearrange("p (c v) -> p c v", c=2)
        Sb = carve(2 * 256 // 2, BF16).rearrange("p (c v) -> p c v", c=2)
        r_S, r_Sb = P.res("S"), P.res("Sb")
        pbuf = [carve(512 // 2, BF16) for _ in range(4)]
        r_pbuf = [P.res("pb%d" % i) for i in range(4)]
        attb = [carve(64, BF16) for _ in range(2)]
        r_attb = [P.res("attb0"), P.res("attb1")]
        qaT = carve(4 * TP // 2, BF16).rearrange("p (c t) -> p c t", c=4)
        r_qaT = P.res("qaT")
        ug = [qaT] * 2
        r_ug = [r_qaT] * 2
        qexp = [carve(2 * 64 // 2, BF16).rearrange("p (c t) -> p c t", c=2) for _ in range(2)]
        kexp = [carve(256 // 2, BF16) for _ in range(2)]
        pexp = [carve(64 // 2, BF16) for _ in range(2)]
        r_qexp = [P.res("qexp0"), P.res("qexp1")]
        r_kexp = [P.res("kexp0"), P.res("kexp1")]
        r_pexp = [P.res("pexp0"), P.res("pexp1")]
        s0 = [carve(512).rearrange("p (c v) -> p c v", c=2) for _ in range(2)]
        s0b = [carve(256, BF16).rearrange("p (c v) -> p c v", c=2) for _ in range(2)]
        r_s0 = [P.res("s0a"), P.res("s0b")]
        r_s0b = [P.res("s0ba"), P.res("s0bb")]
        cst_k = [carve(128) for _ in range(2)]
        cst_v = [carve(128) for _ in range(2)]
        r_cstk = [P.res("cka"), P.res("ckb")]
        r_cstv = [P.res("cva"), P.res("cvb")]
        kcb = [carve(64, BF16) for _ in range(2)]
        kcT = [carve(64, BF16) for _ in range(2)]
        r_kcb = [P.res("kcb0"), P.res("kcb1")]
        r_kcT = [P.res("kcT0"), P.res("kcT1")]
        vcall = mg[:, 1:4, :].rearrange("p t c -> p (t c)")[:, 0:16 * 130].rearrange("p (b c) -> p b c", b=16)
        r_vcall = P.res("vcall")
        kvst = carve(256)
        r_kvst = P.res("kvst")
        gfree = gm[:, 1:4, :].rearrange("p t c -> p (t c)")
        kfree = kd[:, 1:4, :].rearrange("p t c -> p (t c)")
        vfree = vv[:, 1:4, :].rearrange("p t c -> p (t c)")
        s0 = s0 + [gfree[:, 0:512].rearrange("p (c v) -> p c v", c=2), gfree[:, 512:1024].rearrange("p (c v) -> p c v", c=2)]
        r_s0 = r_s0 + [P.res("s0c"), P.res("s0d")]
        s0b = s0b + [kfree[:, 0:512].rearrange("p (c v) -> p c v", c=2), kfree[:, 512:1024].rearrange("p (c v) -> p c v", c=2)]
        r_s0b = r_s0b + [P.res("s0bc"), P.res("s0bd")]
        cst_k = cst_k + [gfree[:, 1024:1152], gfree[:, 1152:1280]]
        cst_v = cst_v + [gfree[:, 1280:1408], gfree[:, 1408:1536]]
        r_cstk = r_cstk + [P.res("ckc"), P.res("ckd")]
        r_cstv = r_cstv + [P.res("cvc"), P.res("cvd")]
        sfree = sgm[:, 1:4, :].rearrange("p t c -> p (t c)")
        s0 = s0 + [sfree[:, 0:512].rearrange("p (c v) -> p c v", c=2), sfree[:, 512:1024].rearrange("p (c v) -> p c v", c=2)]
        r_s0 = r_s0 + [P.res("s0e"), P.res("s0f")]
        s0b = s0b + [sfree[:, 1024:1280].bitcast(BF16).rearrange("p (c v) -> p c v", c=2),
                     sfree[:, 1280:1536].bitcast(BF16).rearrange("p (c v) -> p c v", c=2)]
        r_s0b = r_s0b + [P.res("s0be"), P.res("s0bf")]
        kcb = kcb + [vfree[:, 0:128], vfree[:, 128:256]]
        kcT = kcT + [vfree[:, 256:384], vfree[:, 384:512]]
        r_kcb = r_kcb + [P.res("kcb2"), P.res("kcb3")]
        r_kcT = r_kcT + [P.res("kcT2"), P.res("kcT3")]
        print("arena words used", off[0])

        r_dram = {}

        def rd(name):
            if name not in r_dram:
                r_dram[name] = P.res("dr_" + name)
            return r_dram[name]

        dma("sp", ident, masks_d[4], [], [r_const], r_const)
        cp(identb, ident, [r_const], [r_const])
        dma("sp", mstage, masks_d[5], [], [r_const], r_const)
        cp(onesb, mstage, [r_const], [r_const])
        for i in range(4):
            dma("sp", mstage, masks_d[i], [], [r_const], r_const)
            cp(mk[i], mstage, [r_const], [r_const])
        P.op("dve", lambda E: E.memset(epsc, EPS), [], [r_const])
        for a in range(4):
            dma("sp", dec[:, a, :], dec_d[a], [], [r_const], r_const)
            dma("sp", decs[:, a, :], decs_d[a], [], [r_const], r_const)
        dma("sp", bm, bm_d, [], [r_const], r_const)
        cp(bmb, bm, [r_const], [r_const])
        for b4 in range(4):
            dma("sp", xin[:, 0:256].rearrange("p (b c) -> p b c", b=4), bmT_d[:, b4 * 4:(b4 + 1) * 4, :], [], [r_xin], r_xin)
            cp(bmTb[:, b4 * 4:(b4 + 1) * 4, :], xin[:, 0:256].rearrange("p (b c) -> p b c", b=4), [r_xin], [r_const])
        for l in range(DEPTH):
            dma("sp", qkw[:, l, 0, :], qnw[l:l + 1, :].to_broadcast([128, 128]), [], [r_const], r_const)
            dma("sp", qkw[:, l, 1, :], knw[l:l + 1, :].to_broadcast([128, 128]), [], [r_const], r_const)
            dma("sp", esink[:, l, :], sinks[l:l + 1, :].to_broadcast([128, 16]), [], [r_const], r_const)
        act(esink, esink, AF.Exp, [r_const], [r_const])
        for l in range(DEPTH):
            for s_, src in enumerate((n1w, n2w)):
                dma("sp", xin[0:16, 0:128], src[l], [], [r_xin], r_xin)
                pb, pr = nb()
                tr(pb[:, 0:16], xin[0:16, 0:128], ident[0:16, 0:16], [r_xin, r_const], [pr])
                cp(lnw[:, l, s_, :], pb[:, 0:16], [pr], [r_mod])
        dma("sp", xin[0:17, :], cv, [], [r_xin], r_xin)
        act(tmp[0][0:17, :], xin[0:17, 0:512], AF.Sigmoid, [r_xin], [r_tmp[0]])
        cfull = xT[:, 0:4, :].rearrange("p k t -> p (k t)")
        r_cf = None
        act(cfull[0:17, :], xin[0:17, :], AF.Sigmoid, [r_xin], r_xk[0:4])
        tt(cfull[0:17, :], cfull[0:17, :], xin[0:17, :], ALU.mult, r_xk[0:4] + [r_xin], r_xk[0:4])
        for k in range(16):
            pb, pr = nb()
            tr(pb[:, 0:17], cfull[0:17, k * 128:(k + 1) * 128], ident[0:17, 0:17], r_xk[0:4] + [r_const], [pr])
            cp(scT[:, k, 0:17], pb[:, 0:17], [pr], [r_mod])
        wctr = [0]

        wmode = ["none"]
        wtid = [0]
        wb_pending = []

        def flush_wb():
            while wb_pending:
                s_, tid_, n_, down_ = wb_pending.pop(0)
                if down_:
                    dma("pool", wbf[tid_], wts[s_][:].rearrange("p k c -> p (k c)"), [r_w[s_]], [], r_w[s_])
                else:
                    dma("pool", wbf[tid_].rearrange("p (k c) -> p k c", k=16)[:, :, 0:n_], wts[s_][:, :, 0:n_], [r_w[s_]], [], r_w[s_])

        def wload(src_ap, ncols, extra=None, down=False):
            s = wctr[0] % 2
            wctr[0] += 1
            n = ncols if extra is None else 2 * ncols
            tid = wtid[0]
            if wmode[0] != "none":
                wtid[0] += 1
            mode = wmode[0]
            if mode == "p0":
                mode = "write" if tid % 2 == 0 else "cast"
            elif mode == "p1":
                mode = "read" if tid % 2 == 0 else "write"
            if mode == "read":
                if down:
                    dma("pool", wts[s][:].rearrange("p k c -> p (k c)"), wbf[tid], [], [r_w[s]], r_w[s])
                else:
                    dma("pool", wts[s][:, :, 0:n], wbf[tid].rearrange("p (k c) -> p k c", k=16)[:, :, 0:n], [], [r_w[s]], r_w[s])
                flush_wb()
                return s
            if down:
                wdv_ = wts[s][:].rearrange("p k c -> p (k c)").rearrange("p (f c) -> p f c", f=4)
                dma("pool", wdv_, src_ap.rearrange("(f p) c -> p f c", p=128), [], [r_w[s]], r_w[s])
            else:
                dma("pool", wts[s][:, :, 0:ncols], src_ap.rearrange("(k p) c -> p k c", p=128), [], [r_w[s]], r_w[s])
                if extra is not None:
                    dma("pool", wts[s][:, :, ncols:2 * ncols], extra.rearrange("(k p) c -> p k c", p=128), [], [r_w[s]], r_w[s])
            flush_wb()
            if mode == "write":
                wb_pending.append((s, tid, n, down))
            return s

        for l in range(DEPTH):
            dma("sp", xin[0:96, 0:128], b_ada[l], [], [r_xin], r_xin)
            pb, pr = nb()
            tr(pb[:, 0:96], xin[0:96, 0:128], ident[0:96, 0:96], [r_xin, r_const], [pr])
            cp(tmp[1][:, 0:96], pb[:, 0:96], [pr], [r_tmp[1]])
            for ct in range(24):
                s = wload(w_ada[l][:, ct * 512:(ct + 1) * 512], 512)
                pb, pr = nb()
                for c4 in range(4):
                    for k in range(16):
                        mm(pb[:, c4 * 17:(c4 + 1) * 17], wts[s][:, k, c4 * 128:(c4 + 1) * 128], scT[:, k, 0:17],
                           k == 0, k == 15, [r_w[s], r_mod], [pr])
                tt(mod[:, l, ct * 4:(ct + 1) * 4, :], pb[:, 0:68].rearrange("p (c b) -> p c b", c=4),
                   tmp[1][:, ct * 4:(ct + 1) * 4].unsqueeze(2).to_broadcast([128, 4, 17]), ALU.add,
                   [pr, r_tmp[1]], [r_mod])
            for s_ in range(2):
                kind = 1 if s_ == 0 else 4
                ts(A[:, l, s_], mod[:, l, kind * 16:(kind + 1) * 16, :], 1.0, None, ALU.add, None, [r_mod], [r_mod])
                tt(A[:, l, s_], A[:, l, s_], lnw[:, l, s_, :].unsqueeze(2).to_broadcast([128, 16, 17]), ALU.mult,
                   [r_mod], [r_mod])

        def run_pass(kind, tok0):
            samp = kind == "s"
            T = NB * LS if samp else TP
            M = 64 if samp else 128
            NT = 1 if samp else TP // 128
            xsrc = xs if samp else xp
            ydst = ys if samp else yp
            DEC = decs if samp else dec
            first = (not samp) and tok0 == 0
            last = (not samp) and tok0 + TP == SEQ

            for t in range(NT):
                dma("sp", xin[0:M, :], xsrc[tok0 + t * 128: tok0 + t * 128 + M, :], [], [r_xin], r_xin)
                for kb in range(4):
                    pb, pr = nb()
                    for k4 in range(4):
                        k = kb * 4 + k4
                        tr(pb[:, k4 * 128:k4 * 128 + M], xin[0:M, k * 128:(k + 1) * 128], ident[0:M, 0:M],
                           [r_xin, r_const], [pr])
                    act(xT[:, kb * 4:(kb + 1) * 4, t * 128:t * 128 + M],
                        pb[:].rearrange("p (k c) -> p k c", k=4)[:, :, 0:M], AF.Copy, [pr], r_xk[kb * 4:(kb + 1) * 4])
            for a in range(2):
                for t in range(NT):
                    src = cs_s[a] if samp else cs_p[a][tok0 + t * 128: tok0 + (t + 1) * 128, :]
                    dma("sp", cst[0:M, a, t, :], src, [], [r_cst], r_cst)

            def modulate(l, s_):
                shk = 0 if s_ == 0 else 3
                pb, pr = nb()
                for kb in range(8):
                    act(sq[:, :, 0:T], xT[:, kb * 2:(kb + 1) * 2, 0:T], AF.Square, r_xk[kb * 2:(kb + 1) * 2], [r_sq])
                    for k4 in range(2):
                        mm(pb[:, 0:T], onesb, sq[:, k4, 0:T], kb == 0 and k4 == 0, kb == 7 and k4 == 1,
                           [r_sq, r_const], [pr])
                rstd_from(rsd[:, 0:T], pb[:, 0:T], float(D), [pr, r_const], [r_rsd], tmp[0][:, 0:T], r_tmp[0])
                for k in range(16):
                    i = k % 2
                    tt(tmp[1 + i][:, 0:T], xT[:, k, 0:T], rsd[:, 0:T], ALU.mult, [r_xk[k], r_rsd], [r_tmp[1 + i]])
                    if not samp:
                        act(hT[:, k, 0:T], tmp[1 + i][:, 0:T], AF.Identity, [r_tmp[1 + i], r_mod], [r_hk[k]],
                            scale=A[:, l, s_, k, 0:1], bias=mod[:, l, shk * 16 + k, 0:1])
                    else:
                        v3 = tmp[1 + i][:, 0:T].rearrange("p (b t) -> p b t", t=LS)
                        tt(v3, v3, A[:, l, s_, k, 1:17].unsqueeze(2).to_broadcast([128, NB, LS]), ALU.mult,
                           [r_tmp[1 + i], r_mod], [r_tmp[1 + i]])
                        tt(hT[:, k, 0:T].rearrange("p (b t) -> p b t", t=LS), v3,
                           mod[:, l, shk * 16 + k, 1:17].unsqueeze(2).to_broadcast([128, NB, LS]), ALU.add,
                           [r_tmp[1 + i], r_mod], [r_hk[k]])

            def resid_add(l, gk, k, pb, pr):
                if not samp:
                    stt(xT[:, k, 0:T], pb[:, 0:T], mod[:, l, gk * 16 + k, 0:1], xT[:, k, 0:T], ALU.mult, ALU.add,
                        [pr, r_mod, r_xk[k]], [r_xk[k]])
                else:
                    v3 = tmp[4][:, 0:T].rearrange("p (b t) -> p b t", t=LS)
                    tt(v3, pb[:, 0:T].rearrange("p (b t) -> p b t", t=LS),
                       mod[:, l, gk * 16 + k, 1:17].unsqueeze(2).to_broadcast([128, NB, LS]), ALU.mult,
                       [pr, r_mod], [r_tmp[4]])
                    tt(xT[:, k, 0:T], xT[:, k, 0:T], tmp[4][:, 0:T], ALU.add, [r_xk[k], r_tmp[4]], [r_xk[k]])

            pending = []

            def flush_pending():
                while pending:
                    pending.pop(0)()

            def proj(l, col0, ncols, evac, extra_col0=None):
                ex = None if extra_col0 is None else w_in[l][:, extra_col0:extra_col0 + ncols]
                s = wload(w_in[l][:, col0:col0 + ncols], ncols, ex)
                n = ncols if ex is None else 2 * ncols
                for t in range(NT):
                    pb, pr = nb()
                    for k in range(16):
                        mm(pb[0:M, 0:n], hT[:, k, t * 128:t * 128 + M], wts[s][:, k, 0:n], k == 0, k == 15,
                           [r_hk[k], r_w[s]], [pr])
                    flush_pending()
                    evac(t, pb, pr)
                    yield

            def drive(fgs, bg=None, k=1):
                for fg in fgs:
                    for _ in fg:
                        for _i in range(k):
                            if bg is not None:
                                try:
                                    next(bg)
                                except StopIteration:
                                    bg = None
                flush_pending()
                if bg is not None:
                    for _ in bg:
                        pass

            rotA, rotB = tmp[0][:, 0:256], tmp[0][:, 256:512]
            rot_out = (tmp[2], tmp[1])
            r_rot_out = (r_tmp[2], r_tmp[1])
            bfbuf = (tmp[3][:, 0:256].bitcast(BF16), tmp[3][:, 256:512].bitcast(BF16))

            def rotary(t, pb, pr, out_f32, r_out):
                pv = pb[0:M, :].rearrange("p (h s d) -> p h s d", h=2, s=2)
                ov = out_f32[0:M, :].rearrange("p (h s d) -> p h s d", h=2, s=2)
                cosb = cst[0:M, 0, t, :].unsqueeze(1).to_broadcast([M, 2, 128])
                sinb = cst[0:M, 1, t, :].unsqueeze(1).to_broadcast([M, 2, 128])
                a_ = rotA[0:M, :].rearrange("p (h d) -> p h d", h=2)
                b_ = rotB[0:M, :].rearrange("p (h d) -> p h d", h=2)
                tt(a_, pv[:, :, 0, :], cosb, ALU.mult, [pr, r_cst], [r_rotA])
                tt(b_, pv[:, :, 1, :], sinb, ALU.mult, [pr, r_cst], [r_rotB])
                tt(ov[:, :, 0, :], a_, b_, ALU.subtract, [r_rotA, r_rotB], [r_out])
                tt(a_, pv[:, :, 0, :], sinb, ALU.mult, [pr, r_cst], [r_rotA])
                tt(b_, pv[:, :, 1, :], cosb, ALU.mult, [pr, r_cst], [r_rotB])
                tt(ov[:, :, 1, :], a_, b_, ALU.add, [r_rotA, r_rotB], [r_out])

            def transposes4(src_bf, r_src, dstT, r_dst, t, ncol=4):
                def go():
                    pb, pr = nb()
                    pbb = pb[:].bitcast(BF16)
                    for c in range(ncol):
                        tr(pbb[:, c * 128:c * 128 + M], src_bf[0:M, c * 128:(c + 1) * 128], identb[0:M, 0:M],
                           [r_src, r_const], [pr])
                    if ncol == 4:
                        act(dstT[:, :, t * 128:t * 128 + M], pbb[:, 0:512].rearrange("p (c m) -> p c m", c=4)[:, :, 0:M],
                            AF.Copy, [pr], [r_dst])
                    else:
                        act(dstT[:, 128 + t * 128:128 + t * 128 + M], pbb[:, 0:M], AF.Copy, [pr], [r_dst])
                pending.append(go)

            scale = 128.0 ** -0.5

            def layer(l):
                modulate(l, 0)

                def mk_ret_evacs(u):
                    def ev_q(t, pb, pr):
                        p_ = t % 2
                        rotary(t, pb, pr, rot_out[p_], r_rot_out[p_])
                        tt(bfbuf[p_][0:M, :].rearrange("p (h d) -> p h d", h=2),
                           rot_out[p_][0:M, :].rearrange("p (h d) -> p h d", h=2),
                           DEC[0:M, 0, 2 * u:2 * u + 2].unsqueeze(2).to_broadcast([M, 2, 256]), ALU.mult,
                           [r_rot_out[p_], r_const], [r_bf[p_]])
                        transposes4(bfbuf[p_], r_bf[p_], qdT, r_qdT, t)

                    def ev_k(t, pb, pr):
                        p_ = t % 2
                        rotary(t, pb, pr, rot_out[p_], r_rot_out[p_])
                        tt(bfbuf[p_][0:M, :].rearrange("p (h d) -> p h d", h=2),
                           rot_out[p_][0:M, :].rearrange("p (h d) -> p h d", h=2),
                           DEC[0:M, 1, 2 * u:2 * u + 2].unsqueeze(2).to_broadcast([M, 2, 256]), ALU.mult,
                           [r_rot_out[p_], r_const], [r_bf[p_]])
                        tt(kd[0:M, t, :].rearrange("p (h d) -> p h d", h=2),
                           rot_out[p_][0:M, :].rearrange("p (h d) -> p h d", h=2),
                           DEC[0:M, 2, 2 * u:2 * u + 2].unsqueeze(2).to_broadcast([M, 2, 256]), ALU.mult,
                           [r_rot_out[p_], r_const], [r_kd])
                        transposes4(bfbuf[p_], r_bf[p_], kdT, r_kdT, t)

                    def ev_v(t, pb, pr):
                        act(vv[0:M, t, :], pb[0:M, :], AF.Copy, [pr], [r_vv])

                    def ev_g(t, pb, pr):
                        act(tmp[4][0:M, :], pb[0:M, :], AF.Sigmoid, [pr], [r_tmp[4]])
                        tt(gm[0:M, t, :], tmp[4][0:M, :], pb[0:M, :], ALU.mult, [pr, r_tmp[4]], [r_gm])

                    def ev_m(t, pb, pr):
                        act(tmp[4][0:M, :], pb[0:M, :], AF.Sigmoid, [pr], [r_tmp[4]])
                        tt(gm[0:M, t, :], gm[0:M, t, :], tmp[4][0:M, :], ALU.mult, [r_gm, r_tmp[4]], [r_gm])
                    return ev_q, ev_k, ev_v, ev_g, ev_m

                def ret_norm_gate(po, pro, t, u, hh):
                    act(tmp[5][0:M, 0:256], po[0:M, 0:256], AF.Square, [pro], [r_tmp[5], r_sm[0]], accum_out=sm[0][0:M, 0:1])
                    rstd_from(sm[2][0:M, 0:1], sm[0][0:M, 0:1], 256.0, [r_sm[0], r_const], [r_sm[2]], sm[1][0:M, 0:1], r_sm[1])
                    stt(mg[0:M, t, u * 512 + hh * 256: u * 512 + (hh + 1) * 256], po[0:M, 0:256], sm[2][0:M, 0:1],
                        gm[0:M, t, hh * 256:(hh + 1) * 256], ALU.mult, ALU.mult, [pro, r_sm[2], r_gm], [r_mg])

                def ret_sweep_prompt(u):
                    for hh in range(2):
                        H = 2 * u + hh
                        if first:
                            P.op("dve", lambda E: E.memset(S[:].rearrange("p c v -> p (c v)"), 0.0), [], [r_S])
                        else:
                            dma("sp", S, st_p[l, H].rearrange("(c p) v -> p c v", p=128), [rd("st_p")], [r_S], r_S)
                        cp(Sb, S, [r_S], [r_Sb], eng="act")
                        att = {}

                        def emit_att(t):
                            tsl = slice(t * 128, (t + 1) * 128)
                            pa, pra = nb()
                            for c in range(2):
                                mm(pa[:, 0:128], kdT[:, 2 * hh + c, tsl], qdT[:, 2 * hh + c, tsl], c == 0, c == 1,
                                   [r_kdT, r_qdT], [pra])
                            i = t % 2
                            tt(attb[i], pa[:, 0:128], mk[0], ALU.mult, [pra, r_const], [r_attb[i]])
                        emit_att(0)
                        yield
                        for t in range(NT):
                            tsl = slice(t * 128, (t + 1) * 128)
                            i = t % 2
                            po, pro = nb()
                            mm(po[:, 0:256], attb[i], vv[:, t, hh * 256:(hh + 1) * 256], True, False, [r_attb[i], r_vv], [pro])
                            for c in range(2):
                                mm(po[:, 0:256], qdT[:, 2 * hh + c, tsl], Sb[:, c, :], False, c == 1, [r_qdT, r_Sb], [pro])
                            ret_norm_gate(po, pro, t, u, hh)
                            for c in range(2):
                                pu, pru = nb()
                                mm(pu[:, 0:256], kd[:, t, hh * 256 + c * 128: hh * 256 + (c + 1) * 128],
                                   vv[:, t, hh * 256:(hh + 1) * 256], True, True, [r_kd, r_vv], [pru])
                                stt(S[:, c, :], S[:, c, :], DEC[:, 3, H:H + 1], pu[:, 0:256], ALU.mult, ALU.add,
                                    [r_S, pru, r_const], [r_S])
                            if t + 1 < NT:
                                cp(Sb, S, [r_S], [r_Sb], eng="act")
                                emit_att(t + 1)
                            yield
                        dma("sp", st_p[l, H].rearrange("(c p) v -> p c v", p=128), S, [r_S], [rd("st_p")], r_S)

                def ret_sweep_sample(u):
                    for hh in range(2):
                        H = 2 * u + hh
                        pa, pra = nb()
                        for c in range(2):
                            mm(pa[0:M, 0:M], kdT[:, 2 * hh + c, 0:M], qdT[:, 2 * hh + c, 0:M], c == 0, c == 1,
                               [r_kdT, r_qdT], [pra])
                        tt(attb[0][0:M, 0:M], pa[0:M, 0:M], mk[2][0:M, 0:M], ALU.mult, [pra, r_const], [r_attb[0]])
                        po, pro = psb[7], psr[7]
                        mm(po[0:M, 0:256], attb[0][0:M, 0:M], vv[0:M, 0, hh * 256:(hh + 1) * 256], True, False,
                           [r_attb[0], r_vv], [pro])
                        KS = 6

                        def ld(b):
                            j = b % KS
                            dma("sp", s0[j], st_in[l, b, H].rearrange("(c p) v -> p c v", p=128), [r_qdT], [r_s0[j]], r_s0[j])
                        PD = 3
                        for b in range(PD):
                            ld(b)
                        def prep(b):
                            j = b % KS
                            i = b % 2
                            cp(s0b[j], s0[j], [r_s0[j]], [r_s0b[j]], eng="act")
                            tt(qexp[i], qdT[:, 2 * hh:2 * hh + 2, 0:64], bmTb[:, b, :].unsqueeze(1).to_broadcast([128, 2, 64]),
                               ALU.mult, [r_qdT, r_const], [r_qexp[i]])
                            ts(kexp[i][0:64, :], kd[0:64, 0, hh * 256:(hh + 1) * 256], bm[0:64, b:b + 1], None, ALU.mult, None,
                               [r_kd, r_const], [r_kexp[i]])
                        prep(0)
                        for b in range(NB):
                            if b + PD < NB:
                                ld(b + PD)
                            j = b % KS
                            i = b % 2
                            for c in range(2):
                                mm(po[0:M, 0:256], qexp[i][:, c, :], s0b[j][:, c, :], False,
                                   b == NB - 1 and c == 1, [r_qexp[i], r_s0b[j]], [pro])
                            pus = []
                            for c in range(2):
                                pu, pru = nb()
                                mm(pu[:, 0:256], kexp[i][0:64, c * 128:(c + 1) * 128], vv[0:64, 0, hh * 256:(hh + 1) * 256],
                                   True, True, [r_kexp[i], r_vv], [pru])
                                pus.append((pu, pru))
                            if b + 1 < NB:
                                prep(b + 1)
                            for c in range(2):
                                pu, pru = pus[c]
                                stt(s0[j][:, c, :], s0[j][:, c, :], DEC[:, 3, H:H + 1], pu[:, 0:256], ALU.mult, ALU.add,
                                    [r_s0[j], pru, r_const], [r_s0[j]])
                            dma("sp", st_s[l, b, H].rearrange("(c p) v -> p c v", p=128), s0[j], [r_s0[j]], [rd("st_s")], r_s0[j])
                            if b % 2 == 1 and b < NB - 1:
                                yield
                        ret_norm_gate(po, pro, 0, u, hh)
                    yield

                def mk_swa_evacs(u):
                    def ev_qa(t, pb, pr):
                        p_ = t % 2
                        sqb, r_sqb = rot_out[p_], r_rot_out[p_]
                        act(sqb[0:M, :], pb[0:M, :], AF.Square, [pr], [r_sqb])
                        P.op("dve", lambda E: E.tensor_reduce(out=sm[3][0:M, 0:4], in_=sqb[0:M, :].rearrange("p (g d) -> p g d", g=4),
                                                              axis=AX.X, op=ALU.add), [r_sqb], [r_sm[3]])
                        rstd_from(sm[5][0:M, 0:4], sm[3][0:M, 0:4], 128.0, [r_sm[3], r_const], [r_sm[5]], sm[4][0:M, 0:4], r_sm[4])
                        tt(sqb[0:M, :].rearrange("p (g d) -> p g d", g=4), pb[0:M, :].rearrange("p (g d) -> p g d", g=4),
                           sm[5][0:M, 0:4].unsqueeze(2).to_broadcast([M, 4, 128]), ALU.mult, [pr, r_sm[5]], [r_sqb])
                        tt(bfbuf[p_][0:M, :].rearrange("p (g d) -> p g d", g=4),
                           sqb[0:M, :].rearrange("p (g d) -> p g d", g=4),
                           qkw[0:M, l, 0, :].unsqueeze(1).to_broadcast([M, 4, 128]), ALU.mult, [r_sqb, r_const], [r_bf[p_]])
                        transposes4(bfbuf[p_], r_bf[p_], qaT, r_qaT, t)

                    def ev_kv(t, pb, pr):
                        p_ = t % 2
                        sqb, r_sqb = rot_out[p_], r_rot_out[p_]
                        act(sqb[0:M, 0:128], pb[0:M, 0:128], AF.Square, [pr], [r_sqb, r_sm[3]], accum_out=sm[3][0:M, 0:1])
                        rstd_from(sm[5][0:M, 0:1], sm[3][0:M, 0:1], 128.0, [r_sm[3], r_const], [r_sm[5]], sm[4][0:M, 0:1], r_sm[4])
                        stt(kvst[0:M, 0:128], pb[0:M, 0:128], sm[5][0:M, 0:1], qkw[0:M, l, 1, :], ALU.mult, ALU.mult,
                            [pr, r_sm[5], r_const], [r_kvst])
                        act(kvst[0:M, 128:256], pb[0:M, 128:256], AF.Copy, [pr], [r_kvst])
                        cp(bfbuf[p_][0:M, 0:128], kvst[0:M, 0:128], [r_kvst], [r_bf[p_]])
                        cp(vext[0:M, t + 1, 0:128], kvst[0:M, 128:256], [r_kvst], [r_vext])
                        transposes4(bfbuf[p_], r_bf[p_], kT, r_kT, t, ncol=1)
                        if samp:
                            for b in range(NB):
                                dma("sp", cks[l, b, 124:128, u, :], kvst[b * 4:(b + 1) * 4, 0:128], [r_kvst], [rd("cks")], r_kvst)
                                dma("sp", cvs[l, b, 124:128, u, :], kvst[b * 4:(b + 1) * 4, 128:256], [r_kvst], [rd("cvs")], r_kvst)
                        elif last and t == NT - 1:
                            dma("sp", ckp[l, :, u, :], kvst[:, 0:128], [r_kvst], [rd("ckp")], r_kvst)
                            dma("sp", cvp[l, :, u, :], kvst[:, 128:256], [r_kvst], [rd("cvp")], r_kvst)

                    def ev_ma(t, pb, pr):
                        act(sgm[0:M, t, :], pb[0:M, :], AF.Sigmoid, [pr], [r_sgm])
                    return ev_qa, ev_kv, ev_ma

                def swa_finish(po, pro, t, u, g2, Mq):
                    pov = po[0:Mq, 0:260].rearrange("p (g d) -> p g d", g=2)
                    tt(sm[0][0:Mq, 0:2], pov[:, :, 128], esink[0:Mq, l, 4 * u + 2 * g2: 4 * u + 2 * g2 + 2], ALU.add,
                       [pro, r_const], [r_sm[0]])
                    recip(sm[1][0:Mq, 0:2], sm[0][0:Mq, 0:2], [r_sm[0]], [r_sm[1]])
                    for gg in range(2):
                        g = g2 * 2 + gg
                        stt(tmp[5][0:Mq, gg * 128:(gg + 1) * 128], po[0:Mq, gg * 130:gg * 130 + 128], sm[1][0:Mq, gg:gg + 1],
                            sgm[0:Mq, t, g * 128:(g + 1) * 128], ALU.mult, ALU.mult, [pro, r_sm[1], r_sgm], [r_tmp[5]])
                    c0 = u * 512 + g2 * 256
                    tt(mg[0:Mq, t, c0:c0 + 256], mg[0:Mq, t, c0:c0 + 256], tmp[5][0:Mq, 0:256], ALU.add, [r_mg, r_tmp[5]], [r_mg])

                def swa_sweep_prompt(u):
                    if not first:
                        cp(kT[:, 0:128], kprev[:, l, u, :], [r_carry], [r_kT])
                        cp(vext[:, 0, :], vprev[:, l, u, :], [r_carry], [r_vext])
                    cur = {}

                    def emit_scores(t):
                        tsl = slice(t * 128, (t + 1) * 128)
                        has_prev = not (first and t == 0)
                        blocks = [(1, mk[0])] + ([(0, mk[1])] if has_prev else [])
                        pbs = []
                        for bi, (off_t, msk) in enumerate(blocks):
                            j = (t % 2) * 2 + bi
                            pscore, prs = nb()
                            mm(pscore[:, :], kT[:, (t + off_t) * 128:(t + off_t + 1) * 128], qaT[:, :, tsl], True, True,
                               [r_kT, r_qaT], [prs])
                            act(pbuf[j], pscore[:, :], AF.Exp, [prs], [r_pbuf[j]], scale=scale)
                            tt(pbuf[j].rearrange("p (g q) -> p g q", g=4), pbuf[j].rearrange("p (g q) -> p g q", g=4),
                               msk.unsqueeze(1).to_broadcast([128, 4, 128]), ALU.mult, [r_pbuf[j], r_const], [r_pbuf[j]])
                            pbs.append((j, t + off_t))
                        cur[t] = pbs
                    emit_scores(0)
                    yield
                    for t in range(NT):
                        pbs = cur.pop(t)
                        for g2 in range(2):
                            po, pro = nb()
                            for gg in range(2):
                                g = g2 * 2 + gg
                                for n_, (j, vt) in enumerate(pbs):
                                    mm(po[:, gg * 130:gg * 130 + 129], pbuf[j][:, g * 128:(g + 1) * 128], vext[:, vt, 0:129],
                                       n_ == 0, n_ == len(pbs) - 1, [r_pbuf[j], r_vext], [pro])
                            swa_finish(po, pro, t, u, g2, 128)
                        if t + 1 < NT:
                            emit_scores(t + 1)
                        yield
                    cp(kprev[:, l, u, :], kT[:, TP:TP + 128], [r_kT], [r_carry])
                    cp(vprev[:, l, u, :], vext[:, NT, :], [r_vext], [r_carry])

                def swa_sweep_sample(u):
                    pss, prss = nb()
                    mm(pss[0:64, 0:256], kT[:, 128:192], qaT[:, :, 0:64], True, True, [r_kT, r_qaT], [prss])
                    act(pbuf[0][0:64, 0:256], pss[0:64, 0:256], AF.Exp, [prss], [r_pbuf[0]], scale=scale)
                    tt(pbuf[0][0:64, 0:256].rearrange("p (g q) -> p g q", g=4), pbuf[0][0:64, 0:256].rearrange("p (g q) -> p g q", g=4),
                       mk[2][0:64, 0:64].unsqueeze(1).to_broadcast([64, 4, 64]), ALU.mult, [r_pbuf[0], r_const], [r_pbuf[0]])
                    psp, prsp = psb[7], psr[7]
                    P.op("dve", lambda E: E.memset(vcall[:, :, 128:130], 1.0), [], [r_vcall])
                    KS = 4

                    def ldc(b):
                        i = b % KS
                        dma("sp", cst_k[i], ck_in[l, b, :, u, :], [], [r_cstk[i]], r_cstk[i])
                        dma("sp", cst_v[i], cv_in[l, b, :, u, :], [], [r_cstv[i]], r_cstv[i])
                    for b in range(2):
                        ldc(b)
                    for b in range(NB):
                        if b + 2 < NB:
                            ldc(b + 2)
                        i = b % KS
                        cp(kcb[i], cst_k[i], [r_cstk[i]], [r_kcb[i]])
                        cp(vcall[:, b, 0:128], cst_v[i], [r_cstv[i]], [r_vcall], eng="act")
                        pt_, prt = nb()
                        tr(pt_[:].bitcast(BF16)[:, 0:128], kcb[i], identb, [r_kcb[i], r_const], [prt])
                        act(kcT[i], pt_[:].bitcast(BF16)[:, 0:128], AF.Copy, [prt], [r_kcT[i]])
                        mm(psp[:, 0:256].rearrange("p (g q) -> p g q", g=4)[:, :, b * 4:(b + 1) * 4], kcT[i],
                           qaT[:, :, b * 4:(b + 1) * 4], True, True, [r_kcT[i], r_qaT], [prsp])
                        dma("sp", cks[l, b, 0:124, u, :], cst_k[i][4:128, :], [r_cstk[i]], [rd("cks")], r_cstk[i])
                        dma("sp", cvs[l, b, 0:124, u, :], cst_v[i][4:128, :], [r_cstv[i]], [rd("cvs")], r_cstv[i])
                        if b % 2 == 1:
                            yield
                    act(pbuf[1][:, 0:256], psp[:, 0:256], AF.Exp, [prsp], [r_pbuf[1]], scale=scale)
                    tt(pbuf[1][:, 0:256].rearrange("p (g q) -> p g q", g=4), pbuf[1][:, 0:256].rearrange("p (g q) -> p g q", g=4),
                       mk[3][:, 0:64].unsqueeze(1).to_broadcast([128, 4, 64]), ALU.mult, [r_pbuf[1], r_const], [r_pbuf[1]])
                    for g2 in range(2):
                        po, pro = nb()
                        for gg in range(2):
                            g = g2 * 2 + gg
                            for b in range(NB):
                                i = b % 2
                                tt(pexp[i], pbuf[1][:, g * 64:(g + 1) * 64], bmTb[:, b, :], ALU.mult,
                                   [r_pbuf[1], r_const], [r_pexp[i]])
                                mm(po[0:64, gg * 130:gg * 130 + 129], pexp[i], vcall[:, b, 0:129], b == 0, False,
                                   [r_pexp[i], r_vcall], [pro])
                            mm(po[0:64, gg * 130:gg * 130 + 129], pbuf[0][0:64, g * 64:(g + 1) * 64], vext[0:64, 1, 0:129],
                               False, True, [r_pbuf[0], r_vext], [pro])
                        swa_finish(po, pro, 0, u, g2, 64)
                    yield

                ret_sweep = ret_sweep_sample if samp else ret_sweep_prompt
                swa_sweep = swa_sweep_sample if samp else swa_sweep_prompt
                bg = None
                for u in range(4):
                    ev_q, ev_k, ev_v, ev_g, ev_m = mk_ret_evacs(u)
                    drive([proj(l, OQ + u * 512, 512, ev_q), proj(l, OK_ + u * 512, 512, ev_k),
                           proj(l, OV + u * 512, 512, ev_v), proj(l, OG + u * 512, 512, ev_g),
                           proj(l, OMR + u * 512, 512, ev_m)], bg, k=(3 if samp else 1))
                    bg = ret_sweep(u)
                    ev_qa, ev_kv, ev_ma = mk_swa_evacs(u)
                    P.op("dve", lambda E: E.memset(vext[:, :, 128:130], 1.0), [], [r_vext])
                    drive([proj(l, OQA + u * 512, 512, ev_qa),
                           proj(l, OKA + u * 128, 128, ev_kv, extra_col0=OVA + u * 128),
                           proj(l, OMA + u * 512, 512, ev_ma)], bg, k=(6 if samp else 1))
                    bg = swa_sweep(u)
                drive([], bg)

                for t in range(NT):
                    for kb in range(2):
                        pb, pr = nb()
                        pbb = pb[:].bitcast(BF16)
                        for k8 in range(8):
                            k = kb * 8 + k8
                            tr(pbb[:, k8 * 128:k8 * 128 + M], mg[0:M, t, k * 128:(k + 1) * 128], identb[0:M, 0:M],
                               [r_mg, r_const], [pr])
                        act(hT[:, kb * 8:(kb + 1) * 8, t * 128:t * 128 + M],
                            pbb[:, :].rearrange("p (k m) -> p k m", k=8)[:, :, 0:M], AF.Copy, [pr], r_hk[kb * 8:(kb + 1) * 8])
                for ct in range(4):
                    s = wload(w_out[l][:, ct * 512:(ct + 1) * 512], 512)
                    for c4 in range(4):
                        pb, pr = nb()
                        for k in range(16):
                            mm(pb[:, 0:T], wts[s][:, k, c4 * 128:(c4 + 1) * 128], hT[:, k, 0:T], k == 0, k == 15,
                               [r_w[s], r_hk[k]], [pr])
                        resid_add(l, 2, ct * 4 + c4, pb, pr)
                modulate(l, 1)
                for fg in range(16):
                    s = wload(w_up[l][:, fg * 512:(fg + 1) * 512], 512)
                    ui = fg % 2
                    for c4 in range(4):
                        pb, pr = nb()
                        for k in range(16):
                            mm(pb[:, 0:T], wts[s][:, k, c4 * 128:(c4 + 1) * 128], hT[:, k, 0:T], k == 0, k == 15,
                               [r_w[s], r_hk[k]], [pr])
                        act(tmp[4][:, 0:T], pb[:, 0:T], AF.Relu, [pr], [r_tmp[4]])
                        tt(ug[ui][:, c4, 0:T], tmp[4][:, 0:T], tmp[4][:, 0:T], ALU.mult, [r_tmp[4]], [r_ug[ui]])
                    s2 = wload(w_down[l][fg * 512:(fg + 1) * 512, :], 2048, down=True)
                    wdv = wts[s2][:].rearrange("p k c -> p (k c)").rearrange("p (f c) -> p f c", f=4)
                    for dc in range(16):
                        pb, pr = nb()
                        for f in range(4):
                            mm(pb[:, 0:T], wdv[:, f, dc * 128:(dc + 1) * 128], ug[ui][:, f, 0:T], f == 0, f == 3,
                               [r_w[s2], r_ug[ui]], [pr])
                        resid_add(l, 5, dc, pb, pr)

            for l in range(DEPTH):
                layer(l)

            for t in range(NT):
                for kb in range(4):
                    pb, pr = nb()
                    for k4 in range(4):
                        k = kb * 4 + k4
                        tr(pb[0:M, k4 * 128:(k4 + 1) * 128], xT[:, k, t * 128:t * 128 + M], ident, [r_xk[k], r_const], [pr])
                    act(xin[0:M, kb * 512:(kb + 1) * 512], pb[0:M, :], AF.Copy, [pr], [r_xin])
                dma("sp", ydst[tok0 + t * 128: tok0 + t * 128 + M, :], xin[0:M, :], [r_xin], [rd("y")], r_xin)

        for p_ in range(SEQ // TP):
            wmode[0] = "p0" if p_ == 0 else ("p1" if p_ == 1 else "read")
            wtid[0] = 0
            run_pass("p", p_ * TP)
            if p_ <= 1:
                flush_wb()
                P.barrier()
        P.barrier()
        wmode[0] = "read"
        wtid[0] = 0
        run_pass("s", 0)
        P.barrier()
        P.emit()
        print("stats", P.stats)
    return nc


_CACHE = {}


def kernel(x_prompt, x_sample, c_prompt, c_sample, state_ret, cache_k_win, cache_v_win,
           norm1_w, norm2_w, w_ada, b_ada, w_in, q_norm_w, k_norm_w, sinks, w_out, w_up, w_down):
    f = lambda a: np.ascontiguousarray(np.asarray(a, dtype=np.float32))
    if "nc" not in _CACHE:
        _CACHE["nc"] = build_nc()
        _CACHE["consts"] = _consts()
    nc = _CACHE["nc"]
    cst = _CACHE["consts"]
    x_prompt, x_sample, c_prompt, c_sample = f(x_prompt), f(x_sample), f(c_prompt), f(c_sample)
    state_ret, cache_k_win, cache_v_win = f(state_ret), f(cache_k_win), f(cache_v_win)
    shared = {
        "n1w": f(norm1_w).reshape(DEPTH, 16, 128), "n2w": f(norm2_w).reshape(DEPTH, 16, 128),
        "w_ada": f(w_ada), "b_ada": f(b_ada).reshape(DEPTH, 96, 128), "w_in": f(w_in),
        "qnw": f(q_norm_w), "knw": f(k_norm_w), "sinks": f(sinks),
        "w_out": f(w_out), "w_up": f(w_up), "w_down": f(w_down),
        "cs_p": cst["cs_p"], "cs_s": cst["cs_s"], "dec": cst["dec"], "decs": cst["decs"],
        "masks": cst["masks"], "bm": cst["bm"], "bmT": cst["bmT"],
    }
    in_maps = []
    for c in range(8):
        m = dict(shared)
        m["xp"] = x_prompt[c]
        m["xs"] = x_sample[NB * c:NB * (c + 1)].reshape(NB * LS, D)
        m["cv"] = np.concatenate([c_prompt[c:c + 1], c_sample[NB * c:NB * (c + 1)]], 0)
        m["st_in"] = np.ascontiguousarray(state_ret[:, NB * c:NB * (c + 1)])
        m["ck_in"] = np.ascontiguousarray(cache_k_win[:, NB * c:NB * (c + 1)])
        m["cv_in"] = np.ascontiguousarray(cache_v_win[:, NB * c:NB * (c + 1)])
        in_maps.append(m)
    res = run_bass_kernel_spmd(nc, in_maps, core_ids=list(range(8)))
    R = res.results
    y_p = np.stack([R[c]["yp"] for c in range(8)], 0)
    y_s = np.concatenate([R[c]["ys"].reshape(NB, LS, D) for c in range(8)], 0)
    st_p = np.stack([R[c]["st_p"] for c in range(8)], 1)
    ck_p = np.stack([R[c]["ckp"] for c in range(8)], 1)
    cv_p = np.stack([R[c]["cvp"] for c in range(8)], 1)
    st_s = np.concatenate([R[c]["st_s"] for c in range(8)], 1)
    ck_s = np.concatenate([R[c]["cks"] for c in range(8)], 1)
    cv_s = np.concatenate([R[c]["cvs"] for c in range(8)], 1)
    return (y_p, y_s, st_p, ck_p, cv_p, st_s, ck_s, cv_s)
```
